# Optimizing a Trainium2 kernel written in Bass

```python
import math
import jax, jax.numpy as jnp
from jax import lax
import numpy as np

D_MODEL = 1024
BATCH = 4
SEQ = 8192
DEPTH = 1

N_META = 16
EXPAND = 2
D_MIX = EXPAND * D_MODEL
D_SSD = D_MIX // 2
SSD_HEADDIM = 64
SSD_HEADS = D_SSD // SSD_HEADDIM
SSD_GROUPS = 2
D_STATE = 128
CONV_K = 4
CHUNK = 128
D_ATTN = D_MIX - D_SSD
DIFF_HEADS = 8
DIFF_V_DIM = D_ATTN // DIFF_HEADS
DIFF_QK_DIM = DIFF_V_DIM // 2
Q_BLOCK = CHUNK
CONV_DIM = D_SSD + 2 * SSD_GROUPS * D_STATE
SPLIT_SIZES = (D_SSD, CONV_DIM, SSD_HEADS, D_ATTN, D_ATTN, D_ATTN, D_ATTN)
D_PROJ = sum(SPLIT_SIZES)
N_PAD = CHUNK - N_META
EPS = 1e-5
NEG_INF = -1e30
ALIBI_MAX = 8.0
DT_MIN = 1e-3
DT_MAX = 1e-1

kernel_name = "hymba_ssd_diffattn_alibi_layer"


def rmsnorm(x, g):
    xf = x.astype(jnp.float32)
    y = xf * lax.rsqrt(jnp.mean(xf * xf, axis=-1, keepdims=True) + EPS)
    return (y * g.astype(jnp.float32)).astype(x.dtype)


def left_pad(t, n):
    return jnp.pad(t, [(0, 0), (n, 0)] + [(0, 0)] * (t.ndim - 2))


def causal_dwconv(x, w, b):
    y = lax.conv_general_dilated(
        x, w[:, None, :], window_strides=(1,), padding=[(CONV_K - 1, 0)],
        dimension_numbers=("NWC", "WIO", "NWC"), feature_group_count=x.shape[-1])
    return y + b


def ssd_chunked(xs, dt, a, bmat, cmat):
    b, lp = xs.shape[:2]
    nc = lp // CHUNK
    r = SSD_HEADS // SSD_GROUPS
    xdt = (xs * dt[..., None]).reshape(b, nc, CHUNK, SSD_GROUPS, r, SSD_HEADDIM)
    adt = (dt * a).reshape(b, nc, CHUNK, SSD_GROUPS, r).transpose(0, 1, 3, 4, 2)
    bc = bmat.reshape(b, nc, CHUNK, SSD_GROUPS, D_STATE)
    cc = cmat.reshape(b, nc, CHUNK, SSD_GROUPS, D_STATE)
    acs = jnp.cumsum(adt, axis=-1)
    causal = jnp.tril(jnp.ones((CHUNK, CHUNK), dtype=bool))
    seg = jnp.exp(jnp.where(causal, acs[..., :, None] - acs[..., None, :], -jnp.inf))
    cb = jnp.einsum("bclgn,bcsgn->bcgls", cc, bc)
    y_diag = jnp.einsum("bcgrls,bcsgrp->bclgrp", cb[:, :, :, None] * seg, xdt)
    decay_to_end = jnp.exp(acs[..., -1:] - acs)
    chunk_states = jnp.einsum("bclgn,bcgrl,bclgrp->bcgrpn", bc, decay_to_end, xdt)
    chunk_decay = jnp.exp(acs[..., -1])

    def step(state, inp):
        st, dec = inp
        return dec[..., None, None] * state + st, state

    init = jnp.zeros_like(chunk_states[:, 0])
    _, prev = lax.scan(step, init, (jnp.moveaxis(chunk_states, 1, 0), jnp.moveaxis(chunk_decay, 1, 0)))
    prev = jnp.moveaxis(prev, 0, 1)
    y_off = jnp.einsum("bclgn,bcgrpn,bcgrl->bclgrp", cc, prev, jnp.exp(acs))
    return (y_diag + y_off).reshape(b, lp, SSD_HEADS, SSD_HEADDIM)


def diff_attention(q, k, v, lam, slopes):
    b, lp, _ = q.shape
    q = q.reshape(b, lp, DIFF_HEADS, 2, DIFF_QK_DIM).transpose(0, 2, 3, 1, 4)
    k = k.reshape(b, lp, DIFF_HEADS, 2, DIFF_QK_DIM).transpose(0, 2, 3, 1, 4)
    v = v.reshape(b, lp, DIFF_HEADS, DIFF_V_DIM).transpose(0, 2, 1, 3)
    scale = DIFF_QK_DIM ** -0.5
    outs = []
    for i in range(lp // Q_BLOCK):
        q_lo, kv_hi = i * Q_BLOCK, (i + 1) * Q_BLOCK
        qb = q[:, :, :, q_lo:kv_hi]
        kb = k[:, :, :, :kv_hi]
        vb = v[:, :, :kv_hi]
        s = jnp.einsum("bhjqd,bhjkd->bhjqk", qb, kb).astype(jnp.float32) * scale
        qpos = q_lo + jnp.arange(Q_BLOCK)[:, None]
        kpos = jnp.arange(kv_hi)[None, :]
        dist = (qpos - kpos).astype(jnp.float32)
        allowed = (kpos <= qpos) & (kpos >= N_PAD)
        s = s - slopes[None, :, None, None, None] * dist
        s = jnp.where(allowed, s, NEG_INF)
        p = jax.nn.softmax(s, axis=-1)
        w = p[:, :, 0] - lam * p[:, :, 1]
        outs.append(jnp.einsum("bhqk,bhke->bhqe", w.astype(v.dtype), vb))
    o = jnp.concatenate(outs, axis=2)
    return o.transpose(0, 2, 1, 3)


def setup_inputs(seed: int = 0) -> dict:
    key = jax.random.key(seed)
    ks = jax.random.split(key, 17)
    f32 = jnp.float32
    nrm = jax.random.normal
    x = nrm(ks[0], (BATCH, SEQ, D_MODEL), f32)
    meta = nrm(ks[1], (N_META, D_MODEL), f32)
    norm_g = 1.0 + 0.05 * nrm(ks[2], (DEPTH, D_MODEL), f32)
    w_in = nrm(ks[3], (DEPTH, D_MODEL, D_PROJ), f32) * D_MODEL ** -0.5
    conv_w = nrm(ks[4], (DEPTH, CONV_K, CONV_DIM), f32) * CONV_K ** -0.5
    conv_b = 0.02 * nrm(ks[5], (DEPTH, CONV_DIM), f32)
    u = jax.random.uniform(ks[6], (DEPTH, SSD_HEADS), f32)
    dt0 = jnp.exp(u * (math.log(DT_MAX) - math.log(DT_MIN)) + math.log(DT_MIN))
    dt_bias = dt0 + jnp.log(-jnp.expm1(-dt0))
    a_log = jnp.log(jax.random.uniform(ks[7], (DEPTH, SSD_HEADS), f32, minval=1.0, maxval=16.0))
    d_skip = 1.0 + 0.1 * nrm(ks[8], (DEPTH, SSD_HEADS), f32)
    ssd_norm_g = 1.0 + 0.05 * nrm(ks[9], (DEPTH, D_SSD), f32)
    lambda_q1 = 0.1 * nrm(ks[10], (DEPTH, DIFF_QK_DIM), f32)
    lambda_k1 = 0.1 * nrm(ks[11], (DEPTH, DIFF_QK_DIM), f32)
    lambda_q2 = 0.1 * nrm(ks[12], (DEPTH, DIFF_QK_DIM), f32)
    lambda_k2 = 0.1 * nrm(ks[13], (DEPTH, DIFF_QK_DIM), f32)
    attn_norm_g = 1.0 + 0.05 * nrm(ks[14], (DEPTH, DIFF_V_DIM), f32)
    w_out = nrm(ks[15], (DEPTH, D_MIX, D_MODEL), f32) * D_MIX ** -0.5
    final_norm_g = 1.0 + 0.05 * nrm(ks[16], (D_MODEL,), f32)
    return {"x": x, "meta": meta, "norm_g": norm_g, "w_in": w_in, "conv_w": conv_w,
            "conv_b": conv_b, "dt_bias": dt_bias, "a_log": a_log, "d_skip": d_skip,
            "ssd_norm_g": ssd_norm_g, "lambda_q1": lambda_q1, "lambda_k1": lambda_k1,
            "lambda_q2": lambda_q2, "lambda_k2": lambda_k2, "attn_norm_g": attn_norm_g,
            "w_out": w_out, "final_norm_g": final_norm_g}


def reference(x, meta, norm_g, w_in, conv_w, conv_b, dt_bias, a_log, d_skip, ssd_norm_g,
              lambda_q1, lambda_k1, lambda_q2, lambda_k2, attn_norm_g, w_out, final_norm_g):
    b = x.shape[0]
    h = jnp.concatenate([jnp.broadcast_to(meta[None], (b, N_META, D_MODEL)).astype(x.dtype), x], axis=1)
    L = h.shape[1]
    split_points = list(np.cumsum(SPLIT_SIZES)[:-1])
    slopes = jnp.exp2(-ALIBI_MAX * jnp.arange(1, DIFF_HEADS + 1, dtype=jnp.float32) / DIFF_HEADS)
    for layer in range(DEPTH):
        hn = rmsnorm(h, norm_g[layer])
        proj = hn @ w_in[layer]
        z_ssd, xbc, dt_raw, q, k, v, z_attn = jnp.split(proj, split_points, axis=-1)

        xbc = jax.nn.silu(causal_dwconv(xbc, conv_w[layer], conv_b[layer])).astype(jnp.float32)
        dt = jax.nn.softplus(dt_raw.astype(jnp.float32) + dt_bias[layer].astype(jnp.float32))
        a = -jnp.exp(a_log[layer].astype(jnp.float32))
        xbc_p = left_pad(xbc, N_PAD)
        dt_p = left_pad(dt, N_PAD)
        lp = L + N_PAD
        xs = xbc_p[..., :D_SSD].reshape(b, lp, SSD_HEADS, SSD_HEADDIM)
        bmat = xbc_p[..., D_SSD:D_SSD + SSD_GROUPS * D_STATE].reshape(b, lp, SSD_GROUPS, D_STATE)
        cmat = xbc_p[..., D_SSD + SSD_GROUPS * D_STATE:].reshape(b, lp, SSD_GROUPS, D_STATE)
        y = ssd_chunked(xs, dt_p, a, bmat, cmat)
        y = y + d_skip[layer].astype(jnp.float32)[:, None] * xs
        y = y[:, N_PAD:].reshape(b, L, D_SSD)
        y = y * jax.nn.silu(z_ssd.astype(jnp.float32))
        yg = y.reshape(b, L, SSD_GROUPS, D_SSD // SSD_GROUPS)
        yg = yg * lax.rsqrt(jnp.mean(yg * yg, axis=-1, keepdims=True) + EPS)
        y_ssd = (yg.reshape(b, L, D_SSD) * ssd_norm_g[layer].astype(jnp.float32)).astype(h.dtype)

        lam_init = 0.8 - 0.6 * math.exp(-0.3 * layer)
        lam = (jnp.exp(jnp.sum(lambda_q1[layer].astype(jnp.float32) * lambda_k1[layer].astype(jnp.float32)))
               - jnp.exp(jnp.sum(lambda_q2[layer].astype(jnp.float32) * lambda_k2[layer].astype(jnp.float32)))
               + lam_init)
        o = diff_attention(left_pad(q, N_PAD), left_pad(k, N_PAD), left_pad(v, N_PAD), lam, slopes)
        o = o[:, N_PAD:]
        o = rmsnorm(o, attn_norm_g[layer]).astype(jnp.float32) * (1.0 - lam_init)
        y_attn = (o.reshape(b, L, D_ATTN) * jax.nn.silu(z_attn.astype(jnp.float32))).astype(h.dtype)

        h = h + jnp.concatenate([y_ssd, y_attn], axis=-1) @ w_out[layer]
    return rmsnorm(h[:, N_META:], final_norm_g)
```

```python
import math
from contextlib import ExitStack

import numpy as np
import concourse.bass as bass
import concourse.mybir as mybir
from concourse.bass_utils import run_bass_kernel_spmd

F32 = mybir.dt.float32
BF16 = mybir.dt.bfloat16
F32R = mybir.dt.float32r
AF = mybir.ActivationFunctionType
ALU = mybir.AluOpType

EPS = 1e-5
LATPAD_B = 0.0
NEGBIG = -30000.0
D = 1024
N_META = 16
N_PAD = 112


class Buf:
    __slots__ = ("name", "w", "r")

    def __init__(self, name):
        self.name = name
        self.w = None
        self.r = []


class Slot:
    def __init__(self, t, b, sem):
        self.t, self.b, self.sem = t, b, sem


class Ring:
    def __init__(self, slots):
        self.slots = slots
        self.i = 0

    def next(self):
        s = self.slots[self.i % len(self.slots)]
        self.i += 1
        return s


class Node:
    __slots__ = ("idx", "eng", "fn", "deps", "dur", "lat", "dsem", "ndma", "tok", "succ", "nd", "ready", "done", "tbl")

    def __init__(self, idx, eng, fn, deps, dur, lat, dsem, ndma):
        self.idx, self.eng, self.fn, self.deps = idx, eng, fn, deps
        self.dur, self.lat, self.dsem, self.ndma = dur, lat, dsem, ndma
        self.tok = None
        self.succ = []
        self.nd = 0
        self.ready = 0.0
        self.done = False
        self.tbl = None


class Sched:
    ENG = ("pe", "act", "dve", "pool", "sp")

    def __init__(self, nc, es):
        self.nc = nc
        self.es = es
        self.sems = {}
        self.cnt = {}
        for e in ("pe", "act", "dve", "pool"):
            self.sems[e] = es.enter_context(nc.semaphore("s_" + e))
            self.cnt[e] = 0
        self.nodes = []
        self.seen = {e: {} for e in self.ENG}
        self.pending = {e: [] for e in self.ENG}
        self.nidx = 0
        self.nbuf = 0
        self.final = None
        self.lat_pad = 0.0

    def buf(self, name=None):
        self.nbuf += 1
        return Buf(name or "b%d" % self.nbuf)

    def dma_sem(self, name):
        key = "d_" + name
        assert key not in self.sems
        self.sems[key] = self.es.enter_context(self.nc.semaphore(key))
        self.cnt[key] = 0
        return key

    def op(self, eng, fn, reads=(), writes=(), dsem=None, ndma=1, dur=0.3, lat=0.15, tbl=None):
        deps = set()
        for b in reads:
            if b.w is not None:
                deps.add(b.w)
        for b in writes:
            if b.w is not None:
                deps.add(b.w)
            deps.update(b.r)
        self.nidx += 1
        n = Node(self.nidx, eng, fn, deps, dur, lat + (self.lat_pad if dsem is None else 0.0), dsem, ndma)
        n.tbl = tbl
        for b in writes:
            b.w = n
            b.r = []
        for b in reads:
            b.r.append(n)
        self.nodes.append(n)
        return n

    def barrier(self):
        self._barrier = True

    def final_wait(self, eng="sp"):
        self.final = eng

    def _schedule(self, nodes):
        import heapq
        for n in nodes:
            n.nd = 0
            n.ready = 0.0
        for n in nodes:
            for d in n.deps:
                if not d.done:
                    d.succ.append(n)
                    n.nd += 1
        A = {e: [] for e in self.ENG}
        Bq = {e: [] for e in self.ENG}
        tm = {e: 0.0 for e in self.ENG}
        order = {e: [] for e in self.ENG}
        cur_tbl = [0]
        for n in nodes:
            if n.nd == 0:
                heapq.heappush(A[n.eng], (0.0, n.idx, n))
        left = len(nodes)

        def act_pick(tm_e):
            a, bq = A["act"], Bq["act"]
            opts = []
            for cnd in heapq.nsmallest(12, bq):
                sw = cnd[1].tbl not in (None, cur_tbl[0])
                opts.append((tm_e + (1.3 if sw else 0.0), cnd[0], cnd[1], True))
            for cnd in heapq.nsmallest(6, a):
                sw = cnd[2].tbl not in (None, cur_tbl[0])
                opts.append((max(tm_e, cnd[0]) + (1.3 if sw else 0.0), cnd[1], cnd[2], False))
            if not opts:
                return None
            return min(opts, key=lambda o: (o[0], o[1]))

        while left:
            best = None
            for e in self.ENG:
                a, bq = A[e], Bq[e]
                while a and a[0][0] <= tm[e]:
                    r, i, n = heapq.heappop(a)
                    heapq.heappush(bq, (i, n))
                if e == "act":
                    o = act_pick(tm[e])
                    if o is None:
                        continue
                    cand = (o[0], o[1], e, o)
                elif bq:
                    cand = (tm[e], bq[0][0], e, True)
                elif a:
                    cand = (a[0][0], a[0][1], e, False)
                else:
                    continue
                if best is None or cand[:2] < best[:2]:
                    best = cand
            start, _, e, fromb = best
            if e == "act":
                o = fromb
                n = o[2]
                if o[3]:
                    Bq[e].remove((n.idx, n))
                    heapq.heapify(Bq[e])
                else:
                    A[e].remove((n.ready, n.idx, n))
                    heapq.heapify(A[e])
                if n.tbl is not None:
                    cur_tbl[0] = n.tbl
            elif fromb:
                _, n = heapq.heappop(Bq[e])
            else:
                _, _, n = heapq.heappop(A[e])
            fin = start + n.dur
            tm[e] = fin
            avail = fin + n.lat
            order[e].append(n)
            n.done = True
            left -= 1
            for s_ in n.succ:
                if s_.ready < avail:
                    s_.ready = avail
                s_.nd -= 1
                if s_.nd == 0:
                    heapq.heappush(A[s_.eng], (s_.ready, s_.idx, s_))
            n.succ = []
        self.est = max(tm.values())
        self.est_busy = {e: sum(n.dur for n in order[e]) for e in self.ENG}
        return order

    def emit(self):
        nc = self.nc
        sems = self.sems
        nodes = self.nodes
        self.nodes = []
        order = self._schedule(nodes)
        for e in self.ENG:
            for n in order[e]:
                if n.dsem is not None:
                    self.cnt[n.dsem] += 16 * n.ndma
                    n.tok = (n.dsem, self.cnt[n.dsem])
                else:
                    self.cnt[e] += 1
                    n.tok = (e, self.cnt[e])
        prog = {}
        for e in self.ENG:
            lst = []
            seen = self.seen[e]
            first = True
            for n in order[e]:
                waits = {}
                toks = [d.tok for d in n.deps]
                if first:
                    toks += self.pending[e]
                    self.pending[e] = []
                    first = False
                for (k, v) in toks:
                    if e == "pe" and k == "pe":
                        continue
                    if seen.get(k, 0) >= v:
                        continue
                    if waits.get(k, 0) < v:
                        waits[k] = v
                for k, v in waits.items():
                    seen[k] = v
                inc = (n.dsem, 16) if n.dsem is not None else (e, 1)
                lst.append((list(waits.items()), n.fn, inc))
                n.deps = None
                n.fn = None
            prog[e] = lst
        if getattr(self, "_barrier", False):
            toks = [(k, v) for k, v in self.cnt.items() if v > 0]
            for e in self.ENG:
                self.pending[e].extend(toks)
            self._barrier = False
        if self.final is not None:
            toks = [(k, v) for k, v in self.cnt.items() if v > 0]
            prog[self.final].append((toks, None, None))
            self.final = None

        def run(e, name):
            for waits, fn, inc in prog[name]:
                for k, v in waits:
                    e.wait_ge(sems[k], v)
                if fn is None:
                    continue
                r = fn(e)
                if isinstance(r, (list, tuple)):
                    for ins in r:
                        ins.then_inc(sems[inc[0]], inc[1])
                else:
                    r.then_inc(sems[inc[0]], inc[1])

        with nc.Block() as block:
            @block.tensor
            def _(e):
                run(e, "pe")

            @block.scalar
            def _(e):
                run(e, "act")

            @block.vector
            def _(e):
                run(e, "dve")

            @block.gpsimd
            def _(e):
                run(e, "pool")

            @block.sync
            def _(e):
                run(e, "sp")


def build(NXC, dbg=False):
    NCH = NXC + 1
    NOWN = NXC // 2
    NG = NOWN // 4
    assert NOWN % 4 == 0
    T = NCH * 128
    TO = NOWN * 128
    nc = bass.Bass("TRN2", target_bir_lowering=False)

    def din(name, shape, dt=F32):
        return nc.dram_tensor(name, list(shape), dt, kind="ExternalInput").ap()

    skind = "ExternalOutput" if dbg else "Internal"

    def dscr(name, shape, dt=BF16):
        return nc.dram_tensor(name, list(shape), dt, kind=skind).ap()

    xl = din("xl", [T, D])
    valid_d = din("valid", [128, NCH])
    wA_d = din("wA", [D, 3072])
    wB_d = din("wB", [D, 3600])
    wo_d = din("wo", [2048, D])
    gbc_d = din("gbc", [128, D])
    cw_d = din("cw", [128, 48])
    cbbc_d = din("cbbc", [128, 1280])
    cbcol_d = din("cbcol", [128, 12])
    vecs_d = din("vecs", [128, 48])
    ssdg_d = din("ssdg", [128, 1024])
    attg_d = din("attg", [128, 128])
    fng_d = din("fng", [128, 1024])
    lamv_d = din("lamv", [128, 256])
    ident_d = din("ident", [128, 128])
    trile_d = din("trile", [128, 128])
    sgt_d = din("sgt", [128, 128])
    qaug_d = din("qaug", [8, 2, NG * 512], BF16)
    kbias_d = din("kbias", [128, 8 * NG * NCH])
    ones_d = din("onesk", [2, T], BF16)
    out_d = nc.dram_tensor("out", [TO, D], F32, kind="ExternalOutput").ap()

    HNT = dscr("HNT", [NCH, 128, 1024])
    KT = dscr("KT", [8, 128, T])
    VV = dscr("VV", [8, 128, NCH, 129])
    QT = dscr("QT", [8, 128, TO])
    GG = dscr("GG", [8, 128, NOWN, 128])
    YS = dscr("YS", [NOWN, 128, 1024])
    YA = dscr("YA", [NOWN, 128, 1024])
    WOS = dscr("WOS", [16, 128, 1024])

    with ExitStack() as es:
        S = Sched(nc, es)

        def sb(stack, name, shape, dt):
            return stack.enter_context(nc.sbuf_tensor("s_" + name, list(shape), dt))

        def mkring(stack, name, n, shape, dt, dma=False):
            return Ring([Slot(sb(stack, "%s%d" % (name, i), shape, dt), S.buf("%s%d" % (name, i)),
                              S.dma_sem("%s%d" % (name, i)) if dma else None) for i in range(n)])

        def fsz(ap):
            n = 1
            for d in ap.shape[1:]:
                n *= d
            return n

        def nbytes(ap):
            n = ap.shape[0]
            for d in ap.shape[1:]:
                n *= d
            return n * (2 if ap.dtype == BF16 else 4)

        def DMA(out, in_, reads, writes, dsem, eng="sp"):
            return S.op(eng, lambda e: e.dma_start(out=out, in_=in_), reads, writes, dsem=dsem,
                        dur=0.5, lat=2.0 + nbytes(out) / 120e3)

        def DMAS(pairs, reads, writes, dsem, eng="sp"):
            return S.op(eng, lambda e: [e.dma_start(out=o, in_=i) for (o, i) in pairs], reads, writes,
                        dsem=dsem, ndma=len(pairs), dur=0.5 * len(pairs),
                        lat=2.0 + sum(nbytes(o) for o, _ in pairs) / 120e3)

        def ACT(out, in_, func, reads, writes, **kw):
            tbl = 1 if func == AF.Silu else (0 if func in (AF.Exp, AF.Ln) else None)
            return S.op("act", lambda e: e.activation(out=out, in_=in_, func=func, **kw), reads, writes,
                        dur=0.2 + 0.00083 * fsz(in_), tbl=tbl)

        def vdur(eng, ap):
            return (0.1 + 0.0015 * fsz(ap)) if eng == "dve" else (0.15 + 0.0036 * fsz(ap))

        def CP(eng, out, in_, reads, writes):
            if eng == "act":
                return ACT(out, in_, AF.Copy, reads, writes)
            return S.op(eng, lambda e: e.tensor_copy(out=out, in_=in_), reads, writes, dur=vdur(eng, out))

        def TT(eng, out, in0, in1, op, reads, writes):
            return S.op(eng, lambda e: e.tensor_tensor(out=out, in0=in0, in1=in1, op=op), reads, writes,
                        dur=vdur(eng, out))

        def TSM(eng, out, in0, scalar1, reads, writes):
            eng = "dve"
            return S.op(eng, lambda e: e.tensor_scalar_mul(out=out, in0=in0, scalar1=scalar1), reads, writes,
                        dur=vdur(eng, out))

        def STT(eng, out, in0, scalar, in1, op0, op1, reads, writes):
            eng = "dve"
            return S.op(eng, lambda e: e.scalar_tensor_tensor(out=out, in0=in0, scalar=scalar, in1=in1,
                                                              op0=op0, op1=op1), reads, writes, dur=vdur(eng, out))

        def MSET(eng, ap, val, writes):
            return S.op(eng, lambda e: e.memset(ap, val), (), writes, dur=vdur(eng, ap))

        def MM(mms, reads, writes):
            def fn(e):
                last = None
                for mm in mms:
                    if len(mm) == 6:
                        last = e.matmul(mm[0], mm[1], mm[2], start=mm[3], stop=mm[4], skip_group_check=True)
                    else:
                        last = e.matmul(mm[0], mm[1], mm[2], start=mm[3], stop=mm[4])
                return last
            d = 0.0
            for mm in mms:
                passes = 4 if mm[1].dtype == F32 else 1
                d += passes * max(0.058, fsz(mm[2]) / 2400.0 + 0.004)
            return S.op("pe", fn, reads, writes, dur=d, lat=0.25)

        def TR(items, reads, writes):
            def fn(e):
                last = None
                for (o, i, idn) in items:
                    last = e.transpose(o, i, idn)
                return last
            return S.op("pe", fn, reads, writes, dur=0.07 * len(items), lat=0.3)

        def RSQRT(ss, lnt, rs, n, Bss, Blnt, Brs):
            ACT(lnt, ss, AF.Ln, [Bss] + CONST, [Blnt], scale=1.0 / n, bias=epsb[:, 0:1])
            ACT(rs, lnt, AF.Exp, [Blnt], [Brs], scale=-0.5)

        psbig = [es.enter_context(nc.psum_tensor("psbig%d" % i, [128, 1024], F32)) for i in range(4)]
        psb = [psbig[i // 2][:, (i % 2) * 512:(i % 2 + 1) * 512] for i in range(8)]
        Bps = [S.buf("ps%d" % i) for i in range(8)]

        class PsRing:
            def __init__(self, banks):
                self.banks = banks
                self.i = 0

            def next(self):
                k = self.banks[self.i % len(self.banks)]
                self.i += 1
                return psb[k], Bps[k]

        identf = sb(es, "identf", [128, 128], F32)
        identb = sb(es, "identb", [128, 128], BF16)
        trilef = sb(es, "trilef", [128, 128], F32)
        trileb = sb(es, "trileb", [128, 128], BF16)
        sgtf = sb(es, "sgtf", [128, 128], F32)
        onesf = sb(es, "onesf", [128, 128], F32)
        onesrow = sb(es, "onesrow", [1, 128], BF16)
        epsb = sb(es, "epsb", [128, 1], F32)
        Bconst = S.buf("const")
        csem = S.dma_sem("const")
        DMAS([(identf[:], ident_d), (trilef[:], trile_d), (sgtf[:], sgt_d)], [], [Bconst], csem)
        Bc2 = S.buf("const2")
        CP("dve", identb[:], identf[:], [Bconst], [Bc2])
        CP("dve", trileb[:], trilef[:], [Bconst], [Bc2])
        MSET("dve", onesf[:], 1.0, [Bc2])
        MSET("dve", onesrow[:], 1.0, [Bc2])
        MSET("dve", epsb[:], EPS, [Bc2])
        CONST = [Bconst, Bc2]

        def load_weights(wst, dst, Bdst, src, ncols, nkt, engs=("dve", "pool", "act")):
            i = 0
            for kt in range(nkt):
                pw = wst.slots[0].t.shape[1]
                for c0 in range(0, ncols, pw):
                    w = min(pw, ncols - c0)
                    sl = wst.next()
                    DMA(sl.t[:, 0:w], src[kt * 128:(kt + 1) * 128, c0:c0 + w], [], [sl.b], sl.sem)
                    CP(engs[i % len(engs)], dst[:, kt, c0:c0 + w], sl.t[:, 0:w], [sl.b], [Bdst[kt]])
                    i += 1


        pab = es.enter_context(ExitStack())
        WB = sb(pab, "WB", [128, 8, 3600], BF16)
        BWB = [S.buf("WB%d" % k) for k in range(8)]
        with ExitStack() as pa:
            WA = sb(pa, "WA", [128, 8, 3072], BF16)
            BWA = [S.buf("WA%d" % k) for k in range(8)]
            load_weights(mkring(pa, "wstA", 5, [128, 1024], F32, dma=True), WA, BWA, wA_d, 3072, 8)
            gbc = sb(pa, "gbc", [128, D], F32)
            Bgbc = S.buf("gbc")
            gsem = S.dma_sem("gbc")
            DMA(gbc[:], gbc_d, [], [Bgbc], gsem)
            xring = mkring(pa, "xs", 3, [128, D], F32, dma=True)
            hnring = mkring(pa, "hn", 3, [128, D], BF16)
            hntring = mkring(pa, "hnT", 3, [128, 8, 128], BF16, dma=True)
            ktring = mkring(pa, "kts", 2, [128, 8, 128], BF16, dma=True)
            vring = mkring(pa, "vs", 2, [128, 8, 129], BF16, dma=True)
            qtring = mkring(pa, "qts", 2, [128, 8, 128], BF16, dma=True)
            junk = sb(pa, "junkA", [128, D], BF16)
            Bjunk = S.buf()
            st = sb(pa, "stA", [128, 4], F32)
            Bss, Blnt, Brs = S.buf(), S.buf(), S.buf()
            pr = PsRing(list(range(8)))
            for sl in vring.slots:
                MSET("pool", sl.t[:, :, 128:129], 1.0, [sl.b])

            def prepA(m):
                xs = xring.next()
                DMA(xs.t[:], xl[m * 128:(m + 1) * 128, :], [], [xs.b], xs.sem)
                ACT(junk[:], xs.t[:], AF.Square, [xs.b], [Bjunk, Bss], accum_out=st[:, 0:1])
                RSQRT(st[:, 0:1], st[:, 1:2], st[:, 2:3], D, Bss, Blnt, Brs)
                hn = hnring.next()
                STT("dve", hn.t[:], xs.t[:], st[:, 2:3], gbc[:], ALU.mult, ALU.mult, [xs.b, Brs, Bgbc], [hn.b])
                pt, Bpt = pr.next()
                ptb = pt.bitcast(BF16)
                TR([(ptb[:, k * 128:(k + 1) * 128], hn.t[:, k * 128:(k + 1) * 128], identb[:]) for k in range(8)],
                   [hn.b] + CONST, [Bpt])
                ht = hntring.next()
                CP("act", ht.t[:].rearrange("p a b -> p (a b)"), ptb[:, 0:1024], [Bpt], [ht.b])
                DMA(HNT[m], ht.t[:].rearrange("p a b -> p (a b)"), [ht.b], [], ht.sem)
                return ht

            def mainA(m, ht):
                def proj_fm(col0, ring_, dst_ap):
                    sl = ring_.next()
                    for half in range(2):
                        pk, Bpk = pr.next()
                        for hh in range(4):
                            h = half * 4 + hh
                            MM([(pk[:, hh * 128:(hh + 1) * 128],
                                 WA[:, k, col0 + h * 128: col0 + (h + 1) * 128], ht.t[:, k, :],
                                 k == 0, k == 7) for k in range(8)], [ht.b] + BWA, [Bpk])
                        CP("dve" if half == 0 else "act",
                           sl.t[:, half * 4:(half + 1) * 4, :].rearrange("p a b -> p (a b)"), pk[:, :], [Bpk], [sl.b])
                    DMA(dst_ap, sl.t[:], [sl.b], [], sl.sem)

                proj_fm(0, ktring, KT[:, :, m * 128:(m + 1) * 128].rearrange("h p t -> p h t"))
                vs = vring.next()
                for ng in range(2):
                    pv, Bpv = pr.next()
                    MM([(pv[:, :], ht.t[:, k, :], WA[:, k, 1024 + ng * 512: 1024 + (ng + 1) * 512], k == 0, k == 7)
                        for k in range(8)], [ht.b] + BWA, [Bpv])
                    CP("act" if ng == 0 else "dve", vs.t[:, ng * 4:(ng + 1) * 4, 0:128],
                       pv[:, :].rearrange("p (a b) -> p a b", a=4), [Bpv], [vs.b])
                DMA(VV[:, :, m, :].rearrange("h t e -> t h e"), vs.t[:], [vs.b], [], vs.sem)
                if m >= 2 and m % 2 == 0:
                    jo = m // 2 - 1
                    proj_fm(2048, qtring, QT[:, :, jo * 128:(jo + 1) * 128].rearrange("h p t -> p h t"))

            hts = {}
            for m in range(NCH + 1):
                if m < NCH:
                    hts[m] = prepA(m)
                if m >= 1:
                    mainA(m - 1, hts.pop(m - 1))
            load_weights(mkring(pa, "wstB", 2, [128, 1024], F32, dma=True), WB, BWB, wB_d, 3600, 8, engs=("pool",))
            wost = mkring(pa, "wost", 2, [128, 1024], F32, dma=True)
            wobf = mkring(pa, "wobf", 2, [128, 1024], BF16, dma=True)
            for kt in range(16):
                sl = wost.next()
                DMA(sl.t[:], wo_d[kt * 128:(kt + 1) * 128, :], [], [sl.b], sl.sem)
                ob = wobf.next()
                CP("pool", ob.t[:], sl.t[:], [sl.b], [ob.b])
                DMA(WOS[kt], ob.t[:], [ob.b], [], ob.sem)
            S.barrier()
            S.emit()

        with ExitStack() as pb:
            S.lat_pad = LATPAD_B
            XC, DTC, ZS, ZA = 0, 1536, 1552, 2576
            small = sb(pb, "smallB", [128, 48 + 12 + 48 + NCH], F32)
            cw = small[:, 0:48]
            cbcol = small[:, 48:60]
            vecs = small[:, 60:108]
            validt = small[:, 108:108 + NCH]
            cbbc = sb(pb, "cbbc", [128, 1280], F32)
            ssdg = sb(pb, "ssdg", [128, 1024], F32)
            Bsm = S.buf("smallB")
            smsem = S.dma_sem("smallB")
            DMAS([(cw, cw_d), (cbcol, cbcol_d), (vecs, vecs_d), (validt, valid_d), (cbbc[:], cbbc_d),
                  (ssdg[:], ssdg_d)], [], [Bsm], smsem)
            Bsm2 = S.buf("smallB2")
            dtb = vecs[:, 0:16]
            aneg = sb(pb, "aneg", [128, 16], F32)
            ACT(aneg[:], vecs[:, 16:32], AF.Exp, [Bsm], [Bsm2])
            S.op("dve", lambda e: e.tensor_scalar_mul(out=aneg[:], in0=aneg[:], scalar1=-1.0), [Bsm2], [Bsm2])
            dsk = vecs[:, 32:48]
            Wd = sb(pb, "Wd", [128, 48, 128], BF16)
            for i in range(48):
                TSM("pool" if i % 2 else "dve", Wd[:, i, :], identf[:], cw[:, i:i + 1], [Bsm] + CONST, [Bsm2])
            SM = [Bsm, Bsm2]

            hntring = mkring(pb, "hnTb", 3, [128, 8, 128], BF16, dma=True)
            xrring = mkring(pb, "xr", 2, [128, 12, 131], BF16)
            for sl in xrring.slots:
                MSET("pool", sl.t[:], 0.0, [sl.b])
            xtring = mkring(pb, "xtm", 3, [128, 1024], F32)
            cvring = mkring(pb, "cvt", 2, [128, 512], F32)
            btring = mkring(pb, "btm", 3, [128, 256], BF16)
            bctring = mkring(pb, "bct", 2, [128, 4, 128], BF16)
            dtring = mkring(pb, "dts", 3, [128, 96], F32)
            exring = mkring(pb, "exs", 3, [128, 48], F32)
            xdtdring = mkring(pb, "xdtd", 3, [128, 1024], BF16)
            xdtring = mkring(pb, "xdt", 2, [128, 1024], BF16)
            state = sb(pb, "state", [128, 1024], F32)
            Bstate = S.buf("state")
            MSET("dve", state[:], 0.0, [Bstate])
            sbfring = mkring(pb, "sbf", 2, [128, 1024], BF16)
            MSET("pool", sbfring.slots[0].t[:], 0.0, [sbfring.slots[0].b])
            gzring = mkring(pb, "gz", 2, [128, 1024], F32)
            gsring = mkring(pb, "gs", 2, [128, 1024], BF16, dma=True)
            a4ring = mkring(pb, "a4", 2, [128, 4, 128], F32R)
            triler = sb(pb, "triler", [128, 128], F32R)
            CP("dve", triler[:], trilef[:], CONST, [Bsm2])
            segring = mkring(pb, "seg", 2, [128, 4, 128], F32)
            mtring = mkring(pb, "mt", 2, [128, 16, 128], BF16)
            cbmring = mkring(pb, "cbm", 2, [128, 2, 128], F32)
            yoring = mkring(pb, "yo", 1, [128, 1024], F32)
            yring = mkring(pb, "yy", 1, [128, 1024], F32)
            y2ring = mkring(pb, "y2", 1, [128, 1024], F32)
            ysring = mkring(pb, "yss", 2, [128, 1024], BF16, dma=True)
            junkb = sb(pb, "junkB", [128, 512], BF16)
            Bjunkb = S.buf()
            st2 = sb(pb, "stB", [128, 8], F32)
            Bst2a, Bst2b, Bst2c = S.buf(), S.buf(), S.buf()
            pr = PsRing([0, 1])
            pr2 = PsRing([2, 3, 4, 5, 6, 7])
            prev_xr = xrring.slots[1]
            sbf_cur = sbfring.next()
            for m in range(NCH):
                own = (m >= 2 and m % 2 == 0)
                ht = hntring.next()
                DMA(ht.t[:].rearrange("p a b -> p (a b)"), HNT[m], [], [ht.b], ht.sem)
                xr = xrring.next()
                CP("pool", xr.t[:, :, 0:3], prev_xr.t[:, :, 128:131], [prev_xr.b], [xr.b])
                for b3 in range(3):
                    px, Bpx = pr.next()
                    for cc in range(4):
                        ct = b3 * 4 + cc
                        MM([(px[:, cc * 128:(cc + 1) * 128], WB[:, k, XC + ct * 128: XC + (ct + 1) * 128],
                             ht.t[:, k, :], k == 0, k == 7) for k in range(8)], [ht.b] + BWB, [Bpx])
                    CP("act" if b3 != 1 else "dve", xr.t[:, b3 * 4:(b3 + 1) * 4, 3:131],
                       px[:, :].rearrange("p (a b) -> p a b", a=4), [Bpx], [xr.b])
                prev_xr = xr
                xt = xtring.next()
                bt = btring.next()
                for b3 in range(3):
                    pc, Bpc = pr.next()
                    ncc = 4 if b3 < 2 else 2
                    for cc in range(ncc):
                        ct = b3 * 4 + cc
                        MM([(pc[:, cc * 128:(cc + 1) * 128], xr.t[:, ct, k:k + 128], Wd[:, ct * 4 + k, :], k == 0, k == 3)
                            for k in range(4)], [xr.b] + SM + CONST, [Bpc])
                    cvt = cvring.next()
                    w_ = ncc * 128
                    TT("dve", cvt.t[:, 0:w_], pc[:, 0:w_], cbbc[:, b3 * 512: b3 * 512 + w_], ALU.add, [Bpc] + SM, [cvt.b])
                    if b3 < 2:
                        ACT(xt.t[:, b3 * 512:(b3 + 1) * 512], cvt.t[:, :], AF.Silu, [cvt.b], [xt.b])
                    else:
                        ACT(bt.t[:, :], cvt.t[:, 0:256], AF.Silu, [cvt.b], [bt.b])
                if own:
                    bct = bctring.next()
                    pf, Bpf = pr.next()
                    for cc in range(4):
                        ct = 8 + cc
                        MM([(pf[:, cc * 128:(cc + 1) * 128], Wd[:, ct * 4 + k, :], xr.t[:, ct, k:k + 128], k == 0, k == 3)
                            for k in range(4)], [xr.b] + SM, [Bpf])
                    for cc in range(4):
                        ACT(bct.t[:, cc, :], pf[:, cc * 128:(cc + 1) * 128], AF.Silu, [Bpf] + SM, [bct.b],
                            bias=cbcol[:, 8 + cc: 9 + cc])
                pd, Bpd = pr2.next()
                MM([(pd[:, 0:16], ht.t[:, k, :], WB[:, k, DTC:DTC + 16], k == 0, k == 7) for k in range(8)],
                   [ht.b] + BWB, [Bpd])
                dts = dtring.next()
                dtr, dte_, dtv, adt, w1 = (dts.t[:, 0:16], dts.t[:, 16:32], dts.t[:, 32:48], dts.t[:, 48:64],
                                           dts.t[:, 64:80])
                TT("dve", dtr, pd[:, 0:16], dtb, ALU.add, [Bpd] + SM, [dts.b])
                ACT(dte_, dtr, AF.Exp, [dts.b], [dts.b])
                ACT(dtv, dte_, AF.Ln, [dts.b], [dts.b], bias=1.0)
                TSM("dve", dtv, dtv, validt[:, m:m + 1], [dts.b] + SM, [dts.b])
                TT("dve", adt, dtv, aneg[:], ALU.mult, [dts.b] + SM, [dts.b])
                MM([(pd[:, 16:32], onesf[:], adt, True, True)], [dts.b] + CONST, [Bpd])
                MM([(pd[:, 32:48], trilef[:], adt, True, True)], [dts.b] + CONST, [Bpd])
                MM([(pd[:, 48:64], sgtf[:], adt, True, True)], [dts.b] + CONST, [Bpd])
                ex = exring.next()
                ACT(ex.t[:, 0:48], pd[:, 16:64], AF.Exp, [Bpd], [ex.b])
                cd, eacs, dte = ex.t[:, 0:16], ex.t[:, 16:32], ex.t[:, 32:48]
                TT("dve", w1, dtv, dte, ALU.mult, [dts.b, ex.b], [dts.b])
                xdtd = xdtdring.next()
                TT("pool", xdtd.t[:].rearrange("p (h q) -> p h q", h=16), xt.t[:].rearrange("p (h q) -> p h q", h=16),
                   w1.unsqueeze(2).to_broadcast([128, 16, 64]), ALU.mult, [xt.b, dts.b], [xdtd.b])
                if own:
                    jo = m // 2 - 1
                    gz = gzring.next()
                    gs = gsring.next()
                    for ng in range(4):
                        pz, Bpz = pr.next()
                        MM([(pz[:, :], ht.t[:, k, :], WB[:, k, ZS + ng * 512: ZS + (ng + 1) * 512], k == 0, k == 7)
                            for k in range(8)], [ht.b] + BWB, [Bpz])
                        if ng < 2:
                            ACT(gz.t[:, ng * 512:(ng + 1) * 512], pz[:, :], AF.Silu, [Bpz], [gz.b])
                        else:
                            ACT(gs.t[:, (ng - 2) * 512:(ng - 1) * 512], pz[:, :], AF.Silu, [Bpz], [gs.b])
                    DMA(GG[:, :, jo, :].rearrange("h t e -> t h e"), gs.t[:].rearrange("p (h e) -> p h e", h=8),
                        [gs.b], [], gs.sem)
                    xdt = xdtring.next()
                    TT("pool", xdt.t[:].rearrange("p (h q) -> p h q", h=16), xt.t[:].rearrange("p (h q) -> p h q", h=16),
                       dtv.unsqueeze(2).to_broadcast([128, 16, 64]), ALU.mult, [xt.b, dts.b], [xdt.b])
                    pcb, Bpcb = pr2.next()
                    for g2 in range(2):
                        MM([(pcb[:, g2 * 128:(g2 + 1) * 128], bct.t[:, g2, :], bct.t[:, 2 + g2, :], True, True)],
                           [bct.b], [Bpcb])
                    cbm = cbmring.next()
                    TT("dve", cbm.t[:], pcb[:, 0:256].rearrange("p (a b) -> p a b", a=2),
                       trilef[:].unsqueeze(1).to_broadcast([128, 2, 128]), ALU.mult, [Bpcb] + CONST, [cbm.b])
                    mt = mtring.next()
                    for b4 in range(4):
                        a4 = a4ring.next()
                        for hh in range(4):
                            hcol = adt[:, b4 * 4 + hh: b4 * 4 + hh + 1]
                            if hh % 2 == 0:
                                TSM("dve", a4.t[:, hh, :], sgtf[:], hcol, [dts.b] + CONST, [a4.b])
                            else:
                                ACT(a4.t[:, hh, :], sgtf[:], AF.Copy, [dts.b] + CONST, [a4.b], scale=hcol)
                        pe_, Bpe = pr2.next()
                        for hh in range(4):
                            MM([(pe_[:, hh * 128:(hh + 1) * 128], a4.t[:, hh, :], triler[:], True, True)],
                               [a4.b] + SM, [Bpe])
                        seg = segring.next()
                        ACT(seg.t[:].rearrange("p a b -> p (a b)"), pe_[:, :], AF.Exp, [Bpe], [seg.b])
                        TT("dve", mt.t[:, b4 * 4:(b4 + 1) * 4, :], seg.t[:],
                           cbm.t[:, b4 // 2, :].unsqueeze(1).to_broadcast([128, 4, 128]), ALU.mult,
                           [seg.b, cbm.b], [mt.b])
                    pys = []
                    for half in range(2):
                        py, Bpy = pr2.next()
                        for hh in range(8):
                            h = half * 8 + hh
                            MM([(py[:, hh * 64:(hh + 1) * 64], mt.t[:, h, :], xdt.t[:, h * 64:(h + 1) * 64], True, True)],
                               [mt.b, xdt.b], [Bpy])
                        pys.append((py, Bpy))
                    pos = []
                    for g2 in range(2):
                        po, Bpo = pr2.next()
                        MM([(po[:, :], bct.t[:, 2 + g2, :], sbf_cur.t[:, g2 * 512:(g2 + 1) * 512], True, True)],
                           [bct.b, sbf_cur.b], [Bpo])
                        pos.append((po, Bpo))
                    yo = yoring.next()
                    yy = yring.next()
                    y2 = y2ring.next()
                    for g2 in range(2):
                        sl_ = slice(g2 * 512, (g2 + 1) * 512)
                        TT("dve", yo.t[:, sl_].rearrange("p (h q) -> p h q", h=8),
                           pos[g2][0][:, :].rearrange("p (h q) -> p h q", h=8),
                           eacs[:, g2 * 8:(g2 + 1) * 8].unsqueeze(2).to_broadcast([128, 8, 64]), ALU.mult,
                           [pos[g2][1], ex.b], [yo.b])
                        TT("dve", yy.t[:, sl_], pys[g2][0][:, :], yo.t[:, sl_], ALU.add, [pys[g2][1], yo.b], [yy.b])
                    TT("pool", y2.t[:].rearrange("p (h q) -> p h q", h=16), xt.t[:].rearrange("p (h q) -> p h q", h=16),
                       dsk.unsqueeze(2).to_broadcast([128, 16, 64]), ALU.mult, [xt.b] + SM, [y2.b])
                    TT("pool", yy.t[:], yy.t[:], y2.t[:], ALU.add, [yy.b, y2.b], [yy.b])
                    TT("dve", yy.t[:], yy.t[:], gz.t[:], ALU.mult, [yy.b, gz.b], [yy.b])
                    for g2 in range(2):
                        ACT(junkb[:], yy.t[:, g2 * 512:(g2 + 1) * 512], AF.Square, [yy.b], [Bjunkb, Bst2a],
                            accum_out=st2[:, g2:g2 + 1])
                    RSQRT(st2[:, 0:2], st2[:, 2:4], st2[:, 4:6], 512, Bst2a, Bst2b, Bst2c)
                    ys = ysring.next()
                    for g2 in range(2):
                        sl_ = slice(g2 * 512, (g2 + 1) * 512)
                        STT("dve" if g2 == 0 else "pool", ys.t[:, sl_], yy.t[:, sl_], st2[:, 4 + g2:5 + g2], ssdg[:, sl_],
                            ALU.mult, ALU.mult, [yy.b, Bst2c] + SM, [ys.b])
                    DMA(YS[jo], ys.t[:], [ys.b], [], ys.sem)
                psts = []
                for g2 in range(2):
                    pst, Bpst = pr2.next()
                    MM([(pst[:, :], bt.t[:, g2 * 128:(g2 + 1) * 128], xdtd.t[:, g2 * 512:(g2 + 1) * 512], True, True)],
                       [bt.b, xdtd.b], [Bpst])
                    psts.append((pst, Bpst))
                TT("dve", state[:].rearrange("p (h q) -> p h q", h=16), state[:].rearrange("p (h q) -> p h q", h=16),
                   cd.unsqueeze(2).to_broadcast([128, 16, 64]), ALU.mult, [Bstate, ex.b], [Bstate])
                for g2 in range(2):
                    sl_ = slice(g2 * 512, (g2 + 1) * 512)
                    TT("dve", state[:, sl_], state[:, sl_], psts[g2][0][:, :], ALU.add, [Bstate, psts[g2][1]], [Bstate])
                nxt = m + 1
                if nxt < NCH and nxt >= 2 and nxt % 2 == 0:
                    sbf_cur = sbfring.next()
                    CP("pool", sbf_cur.t[:], state[:], [Bstate], [sbf_cur.b])
            S.barrier()
            S.emit()
        pab.close()

        with ExitStack() as pcx:
            S.lat_pad = 0.0
            smallc = sb(pcx, "smallC", [128, 128 + 256 + 8], F32)
            gnb = smallc[:, 0:128]
            lamv = smallc[:, 128:384]
            lamt = smallc[:, 384:392]
            Bsc = S.buf("smallC")
            scsem = S.dma_sem("smallC")
            DMAS([(gnb, attg_d), (lamv, lamv_d)], [], [Bsc], scsem)
            Bsc2 = S.buf("smallC2")
            S.op("dve", lambda e: e.tensor_scalar_mul(out=gnb, in0=gnb, scalar1=0.8), [Bsc], [Bsc])
            lprod = sb(pcx, "lprod", [128, 128], F32)
            TT("dve", lprod[:, 0:64], lamv[:, 0:64], lamv[:, 64:128], ALU.mult, [Bsc], [Bsc2])
            TT("dve", lprod[:, 64:128], lamv[:, 128:192], lamv[:, 192:256], ALU.mult, [Bsc], [Bsc2])
            junkc = sb(pcx, "junkC", [128, 128], F32)
            ACT(junkc[:, 0:64], lprod[:, 0:64], AF.Copy, [Bsc2], [Bsc2], accum_out=lamt[:, 0:1])
            ACT(junkc[:, 64:128], lprod[:, 64:128], AF.Copy, [Bsc2], [Bsc2], accum_out=lamt[:, 1:2])
            ACT(lamt[:, 2:4], lamt[:, 0:2], AF.Exp, [Bsc2], [Bsc2])
            TT("dve", lamt[:, 4:5], lamt[:, 3:4], lamt[:, 2:3], ALU.subtract, [Bsc2], [Bsc2])
            S.op("dve", lambda e: e.tensor_scalar_add(out=lamt[:, 5:6], in0=lamt[:, 4:5], scalar1=-0.2), [Bsc2], [Bsc2])
            lamneg = lamt[:, 5:6]
            SC = [Bsc, Bsc2]

            KTb = [[sb(pcx, "KTb%d%d" % (p, j), [66, T], BF16) for j in range(2)] for p in range(2)]
            Vb = [sb(pcx, "Vb%d" % p, [128, NCH, 129], BF16) for p in range(2)]
            QTb = [[sb(pcx, "QTb%d%d" % (p, j), [66, TO], BF16) for j in range(2)] for p in range(2)]
            Gb = [sb(pcx, "Gb%d" % p, [128, NOWN, 128], BF16) for p in range(2)]
            KBb = [sb(pcx, "KBb%d" % p, [128, NG * NCH], F32) for p in range(2)]
            Bset = [S.buf("set%d" % p) for p in range(2)]
            setsem = [S.dma_sem("set%d" % p) for p in range(2)]
            BsetV = [S.buf("setV%d" % p) for p in range(2)]
            setsemV = [S.dma_sem("setV%d" % p) for p in range(2)]
            ptring = mkring(pcx, "pt", 4, [128, 2, 512], BF16)
            osb = sb(pcx, "osb", [128, 9, 160], F32)
            Bosb = S.buf("osb")
            fin = sb(pcx, "fin", [128, 32], F32)
            Bfin = [S.buf() for _ in range(5)]
            tmpo = sb(pcx, "tmpo", [128, 4, 128], F32)
            oo = sb(pcx, "oo", [128, 4, 128], F32)
            o2 = sb(pcx, "o2", [128, 4, 128], F32)
            Btmpo, Boo, Bo2 = S.buf(), S.buf(), S.buf()
            yaring = mkring(pcx, "yas", 2, [128, 4, 128], BF16, dma=True)
            spairs = [(2, 4, 5), (3, 6, 7)]
            spi = [0]

            def oacc(i, j):
                idx = i * 2 + j
                return psb[idx // 3][:, (idx % 3) * 160:(idx % 3) * 160 + 129], idx // 3

            def load_head(h):
                p = h % 2
                pairs = []
                for j in range(2):
                    pairs.append((KTb[p][j][0:64, :], KT[h, j * 64:(j + 1) * 64, :]))
                    pairs.append((QTb[p][j][0:64, :], QT[h, j * 64:(j + 1) * 64, :]))
                    pairs.append((QTb[p][j][64:66, :], qaug_d[h]))
                    pairs.append((KTb[p][j][64:66, :], ones_d))
                pairs.append((KBb[p][:], kbias_d[:, h * NG * NCH:(h + 1) * NG * NCH]))
                DMAS(pairs, [], [Bset[p]], setsem[p])
                DMAS([(Vb[p][:], VV[h]), (Gb[p][:], GG[h])], [], [BsetV[p]], setsemV[p])

            load_head(0)
            for h in range(8):
                p = h % 2
                if h + 1 < 8:
                    load_head(h + 1)
                for g in range(NG):
                    m0 = 8 * g + 2
                    steps = []
                    for c in range(m0 + 7):
                        i0 = 0 if c <= m0 else (c - m0 + 1) // 2
                        steps.append((c, i0))
                    LA = 2
                    pend = []
                    started = set()
                    for idx in range(len(steps) + LA):
                        if idx < len(steps):
                            c, i0 = steps[idx]
                            ncol = (4 - i0) * 128
                            big, b0, b1 = spairs[spi[0] % 2]
                            spi[0] += 1
                            MM([(psb[b0][:, 0:ncol] if j == 0 else psb[b1][:, 0:ncol],
                                 KTb[p][j][0:66, c * 128:(c + 1) * 128],
                                 QTb[p][j][0:66, g * 512 + i0 * 128:(g + 1) * 512], True, True) for j in range(2)],
                               [Bset[p]], [Bps[b0], Bps[b1]])
                            pt = ptring.next()
                            ACT(pt.t[:, :, 0:ncol], psbig[big][:, :].rearrange("p (j c) -> p j c", j=2)[:, :, 0:ncol],
                                AF.Exp, [Bps[b0], Bps[b1], Bset[p]], [pt.b], scale=0.125,
                                bias=KBb[p][:, g * NCH + c: g * NCH + c + 1])
                            if c >= m0 and (c - m0) % 2 == 0:
                                TT("pool", pt.t[:, :, 0:128], pt.t[:, :, 0:128],
                                   trileb[:].unsqueeze(1).to_broadcast([128, 2, 128]), ALU.mult, [pt.b] + CONST, [pt.b])
                            pend.append((c, i0, pt))
                        if idx >= LA:
                            c, i0, pt = pend[idx - LA]
                            mms = []
                            wb = set()
                            for j in range(2):
                                for ii in range(4 - i0):
                                    i = i0 + ii
                                    oap, bk = oacc(i, j)
                                    first = bk not in started
                                    started.add(bk)
                                    mms.append((oap, pt.t[:, j, ii * 128:(ii + 1) * 128], Vb[p][:, c, 0:129], first, True, True))
                                    wb.add(Bps[bk])
                            MM(mms, [pt.b, BsetV[p]], list(wb))
                    for bk in range(3):
                        CP("dve", osb[:, bk * 3:(bk + 1) * 3, :].rearrange("p a b -> p (a b)"), psb[bk][:, 0:480],
                           [Bps[bk]], [Bosb])
                    rl = fin[:, 0:8]
                    rln = fin[:, 8:12]
                    ssq = fin[:, 12:16]
                    lnv = fin[:, 16:20]
                    rs = fin[:, 20:24]
                    S.op("dve", lambda e: e.reciprocal(out=rl.unsqueeze(2), in_=osb[:, 0:8, 128:129]), [Bosb], [Bfin[0]])
                    TSM("dve", rln, rl.rearrange("p (a b) -> p a b", b=2)[:, :, 1], lamneg, [Bfin[0]] + SC, [Bfin[1]])
                    for i in range(4):
                        TSM("dve", tmpo[:, i, :], osb[:, 2 * i, 0:128], rl[:, 2 * i:2 * i + 1], [Bosb, Bfin[0]], [Btmpo])
                        STT("dve", oo[:, i, :], osb[:, 2 * i + 1, 0:128], rln[:, i:i + 1], tmpo[:, i, :], ALU.mult, ALU.add,
                            [Bosb, Bfin[1], Btmpo], [Boo])
                    TT("dve", tmpo[:], oo[:], oo[:], ALU.mult, [Boo], [Btmpo])
                    S.op("dve", lambda e: e.reduce_sum(out=ssq, in_=tmpo[:], axis=mybir.AxisListType.X), [Btmpo], [Bfin[2]], dur=0.6, lat=6.0)
                    RSQRT(ssq, lnv, rs, 128, Bfin[2], Bfin[3], Bfin[4])
                    for i in range(4):
                        STT("dve", o2[:, i, :], oo[:, i, :], rs[:, i:i + 1], gnb, ALU.mult, ALU.mult,
                            [Boo, Bfin[4]] + SC, [Bo2])
                    ya = yaring.next()
                    TT("pool", ya.t[:], o2[:], Gb[p][:, 4 * g:4 * g + 4, :], ALU.mult, [Bo2, BsetV[p]], [ya.b])
                    DMA(YA[4 * g:4 * g + 4, :, h * 128:(h + 1) * 128].rearrange("i t e -> t i e"), ya.t[:], [ya.b], [],
                        ya.sem)
            S.barrier()
            S.emit()

        with ExitStack() as pd_:
            WO = sb(pd_, "WO", [128, 16, 1024], BF16)
            BWO = [S.buf("WO%d" % k) for k in range(16)]
            wosem = S.dma_sem("woload")
            for q4 in range(4):
                DMAS([(WO[:, kt, :], WOS[kt]) for kt in range(q4 * 4, q4 * 4 + 4)], [], BWO[q4 * 4:q4 * 4 + 4], wosem)
            fng = sb(pd_, "fng", [128, 1024], F32)
            Bfng = S.buf("fng")
            fsem = S.dma_sem("fng")
            DMA(fng[:], fng_d, [], [Bfng], fsem)
            ysl = mkring(pd_, "ysl", 3, [128, 2048], BF16, dma=True)
            xol = mkring(pd_, "xol", 3, [128, 1024], F32, dma=True)
            ytr = mkring(pd_, "ytr", 3, [128, 16, 128], BF16)
            hs = mkring(pd_, "hs", 2, [128, 1024], F32)
            outr = mkring(pd_, "outr", 2, [128, 1024], F32, dma=True)
            junkd = sb(pd_, "junkD", [128, 1024], BF16)
            Bjunkd = S.buf()
            st3 = sb(pd_, "stD", [128, 4], F32)
            Bs3 = [S.buf() for _ in range(3)]
            pr = PsRing(list(range(8)))
            def prepD(jo):
                m = 2 * jo + 2
                yl = ysl.next()
                DMAS([(yl.t[:, 0:1024], YS[jo]), (yl.t[:, 1024:2048], YA[jo])], [], [yl.b], yl.sem)
                xo = xol.next()
                DMA(xo.t[:], xl[m * 128:(m + 1) * 128, :], [], [xo.b], xo.sem)
                yt = ytr.next()
                for half in range(2):
                    pt_, Bpt_ = pr.next()
                    ptb = pt_.bitcast(BF16)
                    TR([(ptb[:, k * 128:(k + 1) * 128], yl.t[:, (half * 8 + k) * 128:(half * 8 + k + 1) * 128], identb[:])
                        for k in range(8)], [yl.b] + CONST, [Bpt_])
                    CP("act" if half == 0 else "dve", yt.t[:, half * 8:(half + 1) * 8, :].rearrange("p a b -> p (a b)"),
                       ptb[:, 0:1024], [Bpt_], [yt.b])
                return yt, xo

            def mainD(jo, yt, xo):
                hsum = hs.next()
                for ng in range(2):
                    po, Bpo = pr.next()
                    MM([(po[:, :], yt.t[:, k, :], WO[:, k, ng * 512:(ng + 1) * 512], k == 0, k == 15) for k in range(16)],
                       [yt.b] + BWO, [Bpo])
                    TT("dve", hsum.t[:, ng * 512:(ng + 1) * 512], po[:, :], xo.t[:, ng * 512:(ng + 1) * 512], ALU.add,
                       [Bpo, xo.b], [hsum.b])
                ACT(junkd[:], hsum.t[:], AF.Square, [hsum.b], [Bjunkd, Bs3[0]], accum_out=st3[:, 0:1])
                RSQRT(st3[:, 0:1], st3[:, 1:2], st3[:, 2:3], D, Bs3[0], Bs3[1], Bs3[2])
                ot = outr.next()
                STT("dve", ot.t[:, 0:512], hsum.t[:, 0:512], st3[:, 2:3], fng[:, 0:512], ALU.mult, ALU.mult,
                    [hsum.b, Bs3[2], Bfng], [ot.b])
                STT("dve", ot.t[:, 512:1024], hsum.t[:, 512:1024], st3[:, 2:3], fng[:, 512:1024], ALU.mult, ALU.mult,
                    [hsum.b, Bs3[2], Bfng], [ot.b])
                DMA(out_d[jo * 128:(jo + 1) * 128, :], ot.t[:], [ot.b], [], ot.sem)

            prepped = {}
            for jo in range(NOWN + 1):
                if jo < NOWN:
                    prepped[jo] = prepD(jo)
                if jo >= 1:
                    mainD(jo - 1, *prepped.pop(jo - 1))
            S.final_wait("sp")
            S.emit()
    return nc


def _bc(v, n=128):
    return np.ascontiguousarray(np.broadcast_to(np.asarray(v, np.float32).reshape(1, -1), (n, np.size(v))))


def prep(inputs, NXC):
    NCH = NXC + 1
    NOWN = NXC // 2
    NG = NOWN // 4
    f = lambda a: np.asarray(a, np.float32)
    x = f(inputs["x"])
    meta = f(inputs["meta"])
    w_in = f(inputs["w_in"])[0]
    zs, xbc, dtc = w_in[:, 0:1024], w_in[:, 1024:2560], w_in[:, 2560:2576]
    q, k, v, za = w_in[:, 2576:3600], w_in[:, 3600:4624], w_in[:, 4624:5648], w_in[:, 5648:6672]
    wA = np.ascontiguousarray(np.concatenate([k, v, q], axis=1))
    wB = np.ascontiguousarray(np.concatenate([xbc, dtc, zs, za], axis=1))
    wo = np.ascontiguousarray(f(inputs["w_out"])[0])
    gbc = _bc(f(inputs["norm_g"])[0])
    cwf = f(inputs["conv_w"])[0]
    cw = np.ascontiguousarray(cwf.reshape(4, 12, 128).transpose(2, 1, 0).reshape(128, 48))
    cb = f(inputs["conv_b"])[0]
    cbbc = _bc(cb[0:1280])
    cbcol = np.ascontiguousarray(cb.reshape(12, 128).T)
    vecs = np.concatenate([_bc(f(inputs["dt_bias"])[0]), _bc(f(inputs["a_log"])[0]), _bc(f(inputs["d_skip"])[0])], axis=1)
    ssdg = _bc(f(inputs["ssd_norm_g"])[0])
    attg = _bc(f(inputs["attn_norm_g"])[0])
    fng = _bc(f(inputs["final_norm_g"]))
    lamv = np.concatenate([_bc(f(inputs[n])[0]) for n in ("lambda_q1", "lambda_k1", "lambda_q2", "lambda_k2")], axis=1)
    ident = np.eye(128, dtype=np.float32)
    ar = np.arange(128)
    trile = (ar[:, None] <= ar[None, :]).astype(np.float32)
    sgt = (ar[:, None] > ar[None, :]).astype(np.float32)
    slopes = 2.0 ** (-8.0 * np.arange(1, 9) / 8.0)
    col = np.arange(512)
    qaug = np.zeros((8, 2, 512), np.float32)
    for h in range(8):
        qaug[h, 0] = -slopes[h] * 8.0 * (col % 128)
        qaug[h, 1] = -slopes[h] * 8.0 * 256.0 * (col // 128)
    import ml_dtypes
    qaug = np.ascontiguousarray(np.tile(qaug, (1, 1, NG)).astype(ml_dtypes.bfloat16))
    onesk = np.ones((2, NCH * 128), dtype=ml_dtypes.bfloat16)
    shared = dict(onesk=onesk, wA=wA, wB=wB, wo=wo, gbc=gbc, cw=cw, cbbc=cbbc, cbcol=cbcol, vecs=np.ascontiguousarray(vecs),
                  ssdg=ssdg, attg=attg, fng=fng, lamv=np.ascontiguousarray(lamv), ident=ident, trile=trile, sgt=sgt,
                  qaug=qaug)
    in_maps = []
    B = x.shape[0]
    for b in range(B):
        for r in range(2):
            if r == 1:
                xl = np.concatenate([np.zeros((N_PAD, D), np.float32), meta, x[b, 0:NXC * 128]], axis=0)
                nvalid0 = N_PAD
            else:
                xl = np.concatenate([np.zeros((128 + N_PAD, D), np.float32), meta, x[b, 0:(NXC - 1) * 128]], axis=0)
                nvalid0 = 128 + N_PAD
            pos = np.arange(NCH * 128)
            val = (pos >= nvalid0).astype(np.float32)
            valid = np.ascontiguousarray(val.reshape(NCH, 128).T)
            kb = np.zeros((128, 8, NG, NCH), np.float32)
            for g in range(NG):
                m0 = 8 * g + 2
                for c in range(NCH):
                    base = ar[:, None] - 128.0 * (m0 - c)
                    kb[:, :, g, c] = base * slopes[None, :] + np.where(val[c * 128 + ar] > 0, 0.0, NEGBIG)[:, None]
            d = dict(shared)
            d["xl"] = np.ascontiguousarray(xl)
            d["valid"] = valid
            d["kbias"] = np.ascontiguousarray(kb.reshape(128, 8 * NG * NCH))
            in_maps.append(d)
    return in_maps


def assemble(results, B, NXC):
    NOWN = NXC // 2
    out = np.zeros((B, NXC * 128, D), np.float32)
    for b in range(B):
        for r in range(2):
            o = results[b * 2 + r]["out"]
            for jo in range(NOWN):
                xc = 2 * jo + r
                out[b, xc * 128:(xc + 1) * 128] = o[jo * 128:(jo + 1) * 128]
    return out


def kernel(**inputs):
    x = np.asarray(inputs["x"])
    B, SEQ, _ = x.shape
    NXC = SEQ // 128
    nc = build(NXC)
    in_maps = prep(inputs, NXC)
    res = run_bass_kernel_spmd(nc, in_maps, core_ids=list(range(2 * B)))
    return assemble(res.results, B, NXC)
```

```python
import math
from contextlib import ExitStack

import numpy as np
import concourse.bass as bass
import concourse.mybir as mybir
from concourse.bass_utils import run_bass_kernel_spmd

F32 = mybir.dt.float32
BF16 = mybir.dt.bfloat16
F32R = mybir.dt.float32r
AF = mybir.ActivationFunctionType
ALU = mybir.AluOpType

EPS = 1e-5
LATPAD_B = 0.0
NEGBIG = -30000.0
D = 1024
N_META = 16
N_PAD = 112


class Buf:
    __slots__ = ("name", "w", "r")

    def __init__(self, name):
        self.name = name
        self.w = None
        self.r = []


class Slot:
    def __init__(self, t, b, sem):
        self.t, self.b, self.sem = t, b, sem


class Ring:
    def __init__(self, slots):
        self.slots = slots
        self.i = 0

    def next(self):
        s = self.slots[self.i % len(self.slots)]
        self.i += 1
        return s


class Node:
    __slots__ = ("idx", "eng", "fn", "deps", "dur", "lat", "dsem", "ndma", "tok", "succ", "nd", "ready", "done", "tbl")

    def __init__(self, idx, eng, fn, deps, dur, lat, dsem, ndma):
        self.idx, self.eng, self.fn, self.deps = idx, eng, fn, deps
        self.dur, self.lat, self.dsem, self.ndma = dur, lat, dsem, ndma
        self.tok = None
        self.succ = []
        self.nd = 0
        self.ready = 0.0
        self.done = False
        self.tbl = None


class Sched:
    ENG = ("pe", "act", "dve", "pool", "sp")

    def __init__(self, nc, es):
        self.nc = nc
        self.es = es
        self.sems = {}
        self.cnt = {}
        for e in ("pe", "act", "dve", "pool"):
            self.sems[e] = es.enter_context(nc.semaphore("s_" + e))
            self.cnt[e] = 0
        self.nodes = []
        self.seen = {e: {} for e in self.ENG}
        self.pending = {e: [] for e in self.ENG}
        self.nidx = 0
        self.nbuf = 0
        self.final = None
        self.lat_pad = 0.0

    def buf(self, name=None):
        self.nbuf += 1
        return Buf(name or "b%d" % self.nbuf)

    def dma_sem(self, name):
        key = "d_" + name
        assert key not in self.sems
        self.sems[key] = self.es.enter_context(self.nc.semaphore(key))
        self.cnt[key] = 0
        return key

    def op(self, eng, fn, reads=(), writes=(), dsem=None, ndma=1, dur=0.3, lat=0.15, tbl=None):
        deps = set()
        for b in reads:
            if b.w is not None:
                deps.add(b.w)
        for b in writes:
            if b.w is not None:
                deps.add(b.w)
            deps.update(b.r)
        self.nidx += 1
        n = Node(self.nidx, eng, fn, deps, dur, lat + (self.lat_pad if dsem is None else 0.0), dsem, ndma)
        n.tbl = tbl
        for b in writes:
            b.w = n
            b.r = []
        for b in reads:
            b.r.append(n)
        self.nodes.append(n)
        return n

    def barrier(self):
        self._barrier = True

    def final_wait(self, eng="sp"):
        self.final = eng

    def _schedule(self, nodes):
        import heapq
        for n in nodes:
            n.nd = 0
            n.ready = 0.0
        for n in nodes:
            for d in n.deps:
                if not d.done:
                    d.succ.append(n)
                    n.nd += 1
        A = {e: [] for e in self.ENG}
        Bq = {e: [] for e in self.ENG}
        tm = {e: 0.0 for e in self.ENG}
        order = {e: [] for e in self.ENG}
        cur_tbl = [0]
        for n in nodes:
            if n.nd == 0:
                heapq.heappush(A[n.eng], (0.0, n.idx, n))
        left = len(nodes)

        def act_pick(tm_e):
            a, bq = A["act"], Bq["act"]
            opts = []
            for cnd in heapq.nsmallest(12, bq):
                sw = cnd[1].tbl not in (None, cur_tbl[0])
                opts.append((tm_e + (1.3 if sw else 0.0), cnd[0], cnd[1], True))
            for cnd in heapq.nsmallest(6, a):
                sw = cnd[2].tbl not in (None, cur_tbl[0])
                opts.append((max(tm_e, cnd[0]) + (1.3 if sw else 0.0), cnd[1], cnd[2], False))
            if not opts:
                return None
            return min(opts, key=lambda o: (o[0], o[1]))

        while left:
            best = None
            for e in self.ENG:
                a, bq = A[e], Bq[e]
                while a and a[0][0] <= tm[e]:
                    r, i, n = heapq.heappop(a)
                    heapq.heappush(bq, (i, n))
                if e == "act":
                    o = act_pick(tm[e])
                    if o is None:
                        continue
                    cand = (o[0], o[1], e, o)
                elif bq:
                    cand = (tm[e], bq[0][0], e, True)
                elif a:
                    cand = (a[0][0], a[0][1], e, False)
                else:
                    continue
                if best is None or cand[:2] < best[:2]:
                    best = cand
            start, _, e, fromb = best
            if e == "act":
                o = fromb
                n = o[2]
                if o[3]:
                    Bq[e].remove((n.idx, n))
                    heapq.heapify(Bq[e])
                else:
                    A[e].remove((n.ready, n.idx, n))
                    heapq.heapify(A[e])
                if n.tbl is not None:
                    cur_tbl[0] = n.tbl
            elif fromb:
                _, n = heapq.heappop(Bq[e])
            else:
                _, _, n = heapq.heappop(A[e])
            fin = start + n.dur
            tm[e] = fin
            avail = fin + n.lat
            order[e].append(n)
            n.done = True
            left -= 1
            for s_ in n.succ:
                if s_.ready < avail:
                    s_.ready = avail
                s_.nd -= 1
                if s_.nd == 0:
                    heapq.heappush(A[s_.eng], (s_.ready, s_.idx, s_))
            n.succ = []
        self.est = max(tm.values())
        self.est_busy = {e: sum(n.dur for n in order[e]) for e in self.ENG}
        return order

    def emit(self):
        nc = self.nc
        sems = self.sems
        nodes = self.nodes
        self.nodes = []
        order = self._schedule(nodes)
        for e in self.ENG:
            for n in order[e]:
                if n.dsem is not None:
                    self.cnt[n.dsem] += 16 * n.ndma
                    n.tok = (n.dsem, self.cnt[n.dsem])
                else:
                    self.cnt[e] += 1
                    n.tok = (e, self.cnt[e])
        prog = {}
        for e in self.ENG:
            lst = []
            seen = self.seen[e]
            first = True
            for n in order[e]:
                waits = {}
                toks = [d.tok for d in n.deps]
                if first:
                    toks += self.pending[e]
                    self.pending[e] = []
                    first = False
                for (k, v) in toks:
                    if e == "pe" and k == "pe":
                        continue
                    if seen.get(k, 0) >= v:
                        continue
                    if waits.get(k, 0) < v:
                        waits[k] = v
                for k, v in waits.items():
                    seen[k] = v
                inc = (n.dsem, 16) if n.dsem is not None else (e, 1)
                lst.append((list(waits.items()), n.fn, inc))
                n.deps = None
                n.fn = None
            prog[e] = lst
        if getattr(self, "_barrier", False):
            toks = [(k, v) for k, v in self.cnt.items() if v > 0]
            for e in self.ENG:
                self.pending[e].extend(toks)
            self._barrier = False
        if self.final is not None:
            toks = [(k, v) for k, v in self.cnt.items() if v > 0]
            prog[self.final].append((toks, None, None))
            self.final = None

        def run(e, name):
            for waits, fn, inc in prog[name]:
                for k, v in waits:
                    e.wait_ge(sems[k], v)
                if fn is None:
                    continue
                r = fn(e)
                if isinstance(r, (list, tuple)):
                    for ins in r:
                        ins.then_inc(sems[inc[0]], inc[1])
                else:
                    r.then_inc(sems[inc[0]], inc[1])

        with nc.Block() as block:
            @block.tensor
            def _(e):
                run(e, "pe")

            @block.scalar
            def _(e):
                run(e, "act")

            @block.vector
            def _(e):
                run(e, "dve")

            @block.gpsimd
            def _(e):
                run(e, "pool")

            @block.sync
            def _(e):
                run(e, "sp")


def build(NXC, dbg=False):
    NCH = NXC + 1
    NOWN = NXC // 2
    NG = NOWN // 4
    assert NOWN % 4 == 0
    T = NCH * 128
    TO = NOWN * 128
    nc = bass.Bass("TRN2", target_bir_lowering=False)

    def din(name, shape, dt=F32):
        return nc.dram_tensor(name, list(shape), dt, kind="ExternalInput").ap()

    skind = "ExternalOutput" if dbg else "Internal"

    def dscr(name, shape, dt=BF16):
        return nc.dram_tensor(name, list(shape), dt, kind=skind).ap()

    xl = din("xl", [T, D])
    valid_d = din("valid", [128, NCH])
    wA_d = din("wA", [D, 3072])
    wB_d = din("wB", [D, 3600])
    wo_d = din("wo", [2048, D])
    gbc_d = din("gbc", [128, D])
    cw_d = din("cw", [128, 48])
    cbbc_d = din("cbbc", [128, 1280])
    cbcol_d = din("cbcol", [128, 12])
    vecs_d = din("vecs", [128, 48])
    ssdg_d = din("ssdg", [128, 1024])
    attg_d = din("attg", [128, 128])
    fng_d = din("fng", [128, 1024])
    lamv_d = din("lamv", [128, 256])
    ident_d = din("ident", [128, 128])
    trile_d = din("trile", [128, 128])
    sgt_d = din("sgt", [128, 128])
    qaug_d = din("qaug", [8, 2, NG * 512], BF16)
    kbias_d = din("kbias", [128, 8 * NG * NCH])
    ones_d = din("onesk", [2, T], BF16)
    out_d = nc.dram_tensor("out", [TO, D], F32, kind="ExternalOutput").ap()

    HNT = dscr("HNT", [NCH, 128, 1024])
    KT = dscr("KT", [8, 128, T])
    VV = dscr("VV", [8, 128, NCH, 129])
    QT = dscr("QT", [8, 128, TO])
    GG = dscr("GG", [8, 128, NOWN, 128])
    YS = dscr("YS", [NOWN, 128, 1024])
    YA = dscr("YA", [NOWN, 128, 1024])
    WOS = dscr("WOS", [16, 128, 1024])

    with ExitStack() as es:
        S = Sched(nc, es)

        def sb(stack, name, shape, dt):
            return stack.enter_context(nc.sbuf_tensor("s_" + name, list(shape), dt))

        def mkring(stack, name, n, shape, dt, dma=False):
            return Ring([Slot(sb(stack, "%s%d" % (name, i), shape, dt), S.buf("%s%d" % (name, i)),
                              S.dma_sem("%s%d" % (name, i)) if dma else None) for i in range(n)])

        def fsz(ap):
            n = 1
            for d in ap.shape[1:]:
                n *= d
            return n

        def nbytes(ap):
            n = ap.shape[0]
            for d in ap.shape[1:]:
                n *= d
            return n * (2 if ap.dtype == BF16 else 4)

        def DMA(out, in_, reads, writes, dsem, eng="sp"):
            return S.op(eng, lambda e: e.dma_start(out=out, in_=in_), reads, writes, dsem=dsem,
                        dur=0.5, lat=2.0 + nbytes(out) / 120e3)

        def DMAS(pairs, reads, writes, dsem, eng="sp"):
            return S.op(eng, lambda e: [e.dma_start(out=o, in_=i) for (o, i) in pairs], reads, writes,
                        dsem=dsem, ndma=len(pairs), dur=0.5 * len(pairs),
                        lat=2.0 + sum(nbytes(o) for o, _ in pairs) / 120e3)

        def ACT(out, in_, func, reads, writes, **kw):
            tbl = 1 if func == AF.Silu else (0 if func in (AF.Exp, AF.Ln) else None)
            return S.op("act", lambda e: e.activation(out=out, in_=in_, func=func, **kw), reads, writes,
                        dur=0.2 + 0.00083 * fsz(in_), tbl=tbl)

        def vdur(eng, ap):
            return (0.1 + 0.0015 * fsz(ap)) if eng == "dve" else (0.15 + 0.0036 * fsz(ap))

        def CP(eng, out, in_, reads, writes):
            if eng == "act":
                return ACT(out, in_, AF.Copy, reads, writes)
            return S.op(eng, lambda e: e.tensor_copy(out=out, in_=in_), reads, writes, dur=vdur(eng, out))

        def TT(eng, out, in0, in1, op, reads, writes):
            return S.op(eng, lambda e: e.tensor_tensor(out=out, in0=in0, in1=in1, op=op), reads, writes,
                        dur=vdur(eng, out))

        def TSM(eng, out, in0, scalar1, reads, writes):
            eng = "dve"
            return S.op(eng, lambda e: e.tensor_scalar_mul(out=out, in0=in0, scalar1=scalar1), reads, writes,
                        dur=vdur(eng, out))

        def STT(eng, out, in0, scalar, in1, op0, op1, reads, writes):
            eng = "dve"
            return S.op(eng, lambda e: e.scalar_tensor_tensor(out=out, in0=in0, scalar=scalar, in1=in1,
                                                              op0=op0, op1=op1), reads, writes, dur=vdur(eng, out))

        def MSET(eng, ap, val, writes):
            return S.op(eng, lambda e: e.memset(ap, val), (), writes, dur=vdur(eng, ap))

        def MM(mms, reads, writes):
            def fn(e):
                last = None
                for mm in mms:
                    if len(mm) == 6:
                        last = e.matmul(mm[0], mm[1], mm[2], start=mm[3], stop=mm[4], skip_group_check=True)
                    else:
                        last = e.matmul(mm[0], mm[1], mm[2], start=mm[3], stop=mm[4])
                return last
            d = 0.0
            for mm in mms:
                passes = 4 if mm[1].dtype == F32 else 1
                d += passes * max(0.058, fsz(mm[2]) / 2400.0 + 0.004)
            return S.op("pe", fn, reads, writes, dur=d, lat=0.25)

        def TR(items, reads, writes):
            def fn(e):
                last = None
                for (o, i, idn) in items:
                    last = e.transpose(o, i, idn)
                return last
            return S.op("pe", fn, reads, writes, dur=0.07 * len(items), lat=0.3)

        def RSQRT(ss, lnt, rs, n, Bss, Blnt, Brs):
            ACT(lnt, ss, AF.Ln, [Bss] + CONST, [Blnt], scale=1.0 / n, bias=epsb[:, 0:1])
            ACT(rs, lnt, AF.Exp, [Blnt], [Brs], scale=-0.5)

        psbig = [es.enter_context(nc.psum_tensor("psbig%d" % i, [128, 1024], F32)) for i in range(4)]
        psb = [psbig[i // 2][:, (i % 2) * 512:(i % 2 + 1) * 512] for i in range(8)]
        Bps = [S.buf("ps%d" % i) for i in range(8)]

        class PsRing:
            def __init__(self, banks):
                self.banks = banks
                self.i = 0

            def next(self):
                k = self.banks[self.i % len(self.banks)]
                self.i += 1
                return psb[k], Bps[k]

        identf = sb(es, "identf", [128, 128], F32)
        identb = sb(es, "identb", [128, 128], BF16)
        trilef = sb(es, "trilef", [128, 128], F32)
        trileb = sb(es, "trileb", [128, 128], BF16)
        sgtf = sb(es, "sgtf", [128, 128], F32)
        onesf = sb(es, "onesf", [128, 128], F32)
        onesrow = sb(es, "onesrow", [1, 128], BF16)
        epsb = sb(es, "epsb", [128, 1], F32)
        Bconst = S.buf("const")
        csem = S.dma_sem("const")
        DMAS([(identf[:], ident_d), (trilef[:], trile_d), (sgtf[:], sgt_d)], [], [Bconst], csem)
        Bc2 = S.buf("const2")
        CP("dve", identb[:], identf[:], [Bconst], [Bc2])
        CP("dve", trileb[:], trilef[:], [Bconst], [Bc2])
        MSET("dve", onesf[:], 1.0, [Bc2])
        MSET("dve", onesrow[:], 1.0, [Bc2])
        MSET("dve", epsb[:], EPS, [Bc2])
        CONST = [Bconst, Bc2]

        def load_weights(wst, dst, Bdst, src, ncols, nkt, engs=("dve", "pool", "act")):
            i = 0
            for kt in range(nkt):
                pw = wst.slots[0].t.shape[1]
                for c0 in range(0, ncols, pw):
                    w = min(pw, ncols - c0)
                    sl = wst.next()
                    DMA(sl.t[:, 0:w], src[kt * 128:(kt + 1) * 128, c0:c0 + w], [], [sl.b], sl.sem)
                    CP(engs[i % len(engs)], dst[:, kt, c0:c0 + w], sl.t[:, 0:w], [sl.b], [Bdst[kt]])
                    i += 1


        pab = es.enter_context(ExitStack())
        WB = sb(pab, "WB", [128, 8, 3600], BF16)
        BWB = [S.buf("WB%d" % k) for k in range(8)]
        with ExitStack() as pa:
            WA = sb(pa, "WA", [128, 8, 3072], BF16)
            BWA = [S.buf("WA%d" % k) for k in range(8)]
            load_weights(mkring(pa, "wstA", 5, [128, 1024], F32, dma=True), WA, BWA, wA_d, 3072, 8)
            gbc = sb(pa, "gbc", [128, D], F32)
            Bgbc = S.buf("gbc")
            gsem = S.dma_sem("gbc")
            DMA(gbc[:], gbc_d, [], [Bgbc], gsem)
            xring = mkring(pa, "xs", 3, [128, D], F32, dma=True)
            hnring = mkring(pa, "hn", 3, [128, D], BF16)
            hntring = mkring(pa, "hnT", 3, [128, 8, 128], BF16, dma=True)
            ktring = mkring(pa, "kts", 2, [128, 8, 128], BF16, dma=True)
            vring = mkring(pa, "vs", 2, [128, 8, 129], BF16, dma=True)
            qtring = mkring(pa, "qts", 2, [128, 8, 128], BF16, dma=True)
            junk = sb(pa, "junkA", [128, D], BF16)
            Bjunk = S.buf()
            st = sb(pa, "stA", [128, 4], F32)
            Bss, Blnt, Brs = S.buf(), S.buf(), S.buf()
            pr = PsRing(list(range(8)))
            for sl in vring.slots:
                MSET("pool", sl.t[:, :, 128:129], 1.0, [sl.b])

            def prepA(m):
                xs = xring.next()
                DMA(xs.t[:], xl[m * 128:(m + 1) * 128, :], [], [xs.b], xs.sem)
                ACT(junk[:], xs.t[:], AF.Square, [xs.b], [Bjunk, Bss], accum_out=st[:, 0:1])
                RSQRT(st[:, 0:1], st[:, 1:2], st[:, 2:3], D, Bss, Blnt, Brs)
                hn = hnring.next()
                STT("dve", hn.t[:], xs.t[:], st[:, 2:3], gbc[:], ALU.mult, ALU.mult, [xs.b, Brs, Bgbc], [hn.b])
                pt, Bpt = pr.next()
                ptb = pt.bitcast(BF16)
                TR([(ptb[:, k * 128:(k + 1) * 128], hn.t[:, k * 128:(k + 1) * 128], identb[:]) for k in range(8)],
                   [hn.b] + CONST, [Bpt])
                ht = hntring.next()
                CP("act", ht.t[:].rearrange("p a b -> p (a b)"), ptb[:, 0:1024], [Bpt], [ht.b])
                DMA(HNT[m], ht.t[:].rearrange("p a b -> p (a b)"), [ht.b], [], ht.sem)
                return ht

            def mainA(m, ht):
                def proj_fm(col0, ring_, dst_ap):
                    sl = ring_.next()
                    for half in range(2):
                        pk, Bpk = pr.next()
                        for hh in range(4):
                            h = half * 4 + hh
                            MM([(pk[:, hh * 128:(hh + 1) * 128],
                                 WA[:, k, col0 + h * 128: col0 + (h + 1) * 128], ht.t[:, k, :],
                                 k == 0, k == 7) for k in range(8)], [ht.b] + BWA, [Bpk])
                        CP("dve" if half == 0 else "act",
                           sl.t[:, half * 4:(half + 1) * 4, :].rearrange("p a b -> p (a b)"), pk[:, :], [Bpk], [sl.b])
                    DMA(dst_ap, sl.t[:], [sl.b], [], sl.sem)

                proj_fm(0, ktring, KT[:, :, m * 128:(m + 1) * 128].rearrange("h p t -> p h t"))
                vs = vring.next()
                for ng in range(2):
                    pv, Bpv = pr.next()
                    MM([(pv[:, :], ht.t[:, k, :], WA[:, k, 1024 + ng * 512: 1024 + (ng + 1) * 512], k == 0, k == 7)
                        for k in range(8)], [ht.b] + BWA, [Bpv])
                    CP("act" if ng == 0 else "dve", vs.t[:, ng * 4:(ng + 1) * 4, 0:128],
                       pv[:, :].rearrange("p (a b) -> p a b", a=4), [Bpv], [vs.b])
                DMA(VV[:, :, m, :].rearrange("h t e -> t h e"), vs.t[:], [vs.b], [], vs.sem)
                if m >= 2 and m % 2 == 0:
                    jo = m // 2 - 1
                    proj_fm(2048, qtring, QT[:, :, jo * 128:(jo + 1) * 128].rearrange("h p t -> p h t"))

            hts = {}
            for m in range(NCH + 1):
                if m < NCH:
                    hts[m] = prepA(m)
                if m >= 1:
                    mainA(m - 1, hts.pop(m - 1))
            load_weights(mkring(pa, "wstB", 2, [128, 1024], F32, dma=True), WB, BWB, wB_d, 3600, 8, engs=("pool",))
            wost = mkring(pa, "wost", 2, [128, 1024], F32, dma=True)
            wobf = mkring(pa, "wobf", 2, [128, 1024], BF16, dma=True)
            for kt in range(16):
                sl = wost.next()
                DMA(sl.t[:], wo_d[kt * 128:(kt + 1) * 128, :], [], [sl.b], sl.sem)
                ob = wobf.next()
                CP("pool", ob.t[:], sl.t[:], [sl.b], [ob.b])
                DMA(WOS[kt], ob.t[:], [ob.b], [], ob.sem)
            S.barrier()
            S.emit()

        with ExitStack() as pb:
            S.lat_pad = LATPAD_B
            XC, DTC, ZS, ZA = 0, 1536, 1552, 2576
            small = sb(pb, "smallB", [128, 48 + 12 + 48 + NCH], F32)
            cw = small[:, 0:48]
            cbcol = small[:, 48:60]
            vecs = small[:, 60:108]
            validt = small[:, 108:108 + NCH]
            cbbc = sb(pb, "cbbc", [128, 1280], F32)
            ssdg = sb(pb, "ssdg", [128, 1024], F32)
            Bsm = S.buf("smallB")
            smsem = S.dma_sem("smallB")
            DMAS([(cw, cw_d), (cbcol, cbcol_d), (vecs, vecs_d), (validt, valid_d), (cbbc[:], cbbc_d),
                  (ssdg[:], ssdg_d)], [], [Bsm], smsem)
            Bsm2 = S.buf("smallB2")
            dtb = vecs[:, 0:16]
            aneg = sb(pb, "aneg", [128, 16], F32)
            ACT(aneg[:], vecs[:, 16:32], AF.Exp, [Bsm], [Bsm2])
            S.op("dve", lambda e: e.tensor_scalar_mul(out=aneg[:], in0=aneg[:], scalar1=-1.0), [Bsm2], [Bsm2])
            dsk = vecs[:, 32:48]
            Wd = sb(pb, "Wd", [128, 48, 128], BF16)
            for i in range(48):
                TSM("pool" if i % 2 else "dve", Wd[:, i, :], identf[:], cw[:, i:i + 1], [Bsm] + CONST, [Bsm2])
            SM = [Bsm, Bsm2]

            hntring = mkring(pb, "hnTb", 3, [128, 8, 128], BF16, dma=True)
            xrring = mkring(pb, "xr", 2, [128, 12, 131], BF16)
            for sl in xrring.slots:
                MSET("pool", sl.t[:], 0.0, [sl.b])
            xtring = mkring(pb, "xtm", 3, [128, 1024], F32)
            cvring = mkring(pb, "cvt", 2, [128, 512], F32)
            btring = mkring(pb, "btm", 3, [128, 256], BF16)
            bctring = mkring(pb, "bct", 2, [128, 4, 128], BF16)
            dtring = mkring(pb, "dts", 3, [128, 96], F32)
            exring = mkring(pb, "exs", 3, [128, 48], F32)
            xdtdring = mkring(pb, "xdtd", 3, [128, 1024], BF16)
            xdtring = mkring(pb, "xdt", 2, [128, 1024], BF16)
            state = sb(pb, "state", [128, 1024], F32)
            Bstate = S.buf("state")
            MSET("dve", state[:], 0.0, [Bstate])
            sbfring = mkring(pb, "sbf", 2, [128, 1024], BF16)
            MSET("pool", sbfring.slots[0].t[:], 0.0, [sbfring.slots[0].b])
            gzring = mkring(pb, "gz", 2, [128, 1024], F32)
            gsring = mkring(pb, "gs", 2, [128, 1024], BF16, dma=True)
            a4ring = mkring(pb, "a4", 2, [128, 4, 128], F32R)
            triler = sb(pb, "triler", [128, 128], F32R)
            CP("dve", triler[:], trilef[:], CONST, [Bsm2])
            segring = mkring(pb, "seg", 2, [128, 4, 128], F32)
            mtring = mkring(pb, "mt", 2, [128, 16, 128], BF16)
            cbmring = mkring(pb, "cbm", 2, [128, 2, 128], F32)
            yoring = mkring(pb, "yo", 1, [128, 1024], F32)
            yring = mkring(pb, "yy", 1, [128, 1024], F32)
            y2ring = mkring(pb, "y2", 1, [128, 1024], F32)
            ysring = mkring(pb, "yss", 2, [128, 1024], BF16, dma=True)
            junkb = sb(pb, "junkB", [128, 512], BF16)
            Bjunkb = S.buf()
            st2 = sb(pb, "stB", [128, 8], F32)
            Bst2a, Bst2b, Bst2c = S.buf(), S.buf(), S.buf()
            pr = PsRing([0, 1, 2])
            pr2 = PsRing([3, 4, 5, 6, 7])
            prev_xr = xrring.slots[1]
            sbf_cur = sbfring.next()
            for m in range(NCH):
                own = (m >= 2 and m % 2 == 0)
                ht = hntring.next()
                DMA(ht.t[:].rearrange("p a b -> p (a b)"), HNT[m], [], [ht.b], ht.sem)
                xr = xrring.next()
                CP("pool", xr.t[:, :, 0:3], prev_xr.t[:, :, 128:131], [prev_xr.b], [xr.b])
                for b3 in range(3):
                    px, Bpx = pr.next()
                    for cc in range(4):
                        ct = b3 * 4 + cc
                        MM([(px[:, cc * 128:(cc + 1) * 128], WB[:, k, XC + ct * 128: XC + (ct + 1) * 128],
                             ht.t[:, k, :], k == 0, k == 7) for k in range(8)], [ht.b] + BWB, [Bpx])
                    CP("act" if b3 != 1 else "dve", xr.t[:, b3 * 4:(b3 + 1) * 4, 3:131],
                       px[:, :].rearrange("p (a b) -> p a b", a=4), [Bpx], [xr.b])
                prev_xr = xr
                xt = xtring.next()
                bt = btring.next()
                for b3 in range(3):
                    pc, Bpc = pr.next()
                    ncc = 4 if b3 < 2 else 2
                    for cc in range(ncc):
                        ct = b3 * 4 + cc
                        MM([(pc[:, cc * 128:(cc + 1) * 128], xr.t[:, ct, k:k + 128], Wd[:, ct * 4 + k, :], k == 0, k == 3)
                            for k in range(4)], [xr.b] + SM + CONST, [Bpc])
                    cvt = cvring.next()
                    w_ = ncc * 128
                    TT("dve", cvt.t[:, 0:w_], pc[:, 0:w_], cbbc[:, b3 * 512: b3 * 512 + w_], ALU.add, [Bpc] + SM, [cvt.b])
                    if b3 < 2:
                        ACT(xt.t[:, b3 * 512:(b3 + 1) * 512], cvt.t[:, :], AF.Silu, [cvt.b], [xt.b])
                    else:
                        ACT(bt.t[:, :], cvt.t[:, 0:256], AF.Silu, [cvt.b], [bt.b])
                if own:
                    bct = bctring.next()
                    pf, Bpf = pr.next()
                    for cc in range(4):
                        ct = 8 + cc
                        MM([(pf[:, cc * 128:(cc + 1) * 128], Wd[:, ct * 4 + k, :], xr.t[:, ct, k:k + 128], k == 0, k == 3)
                            for k in range(4)], [xr.b] + SM, [Bpf])
                    for cc in range(4):
                        ACT(bct.t[:, cc, :], pf[:, cc * 128:(cc + 1) * 128], AF.Silu, [Bpf] + SM, [bct.b],
                            bias=cbcol[:, 8 + cc: 9 + cc])
                pd, Bpd = pr2.next()
                MM([(pd[:, 0:16], ht.t[:, k, :], WB[:, k, DTC:DTC + 16], k == 0, k == 7) for k in range(8)],
                   [ht.b] + BWB, [Bpd])
                dts = dtring.next()
                dtr, dte_, dtv, adt, w1 = (dts.t[:, 0:16], dts.t[:, 16:32], dts.t[:, 32:48], dts.t[:, 48:64],
                                           dts.t[:, 64:80])
                TT("dve", dtr, pd[:, 0:16], dtb, ALU.add, [Bpd] + SM, [dts.b])
                ACT(dte_, dtr, AF.Exp, [dts.b], [dts.b])
                ACT(dtv, dte_, AF.Ln, [dts.b], [dts.b], bias=1.0)
                TSM("dve", dtv, dtv, validt[:, m:m + 1], [dts.b] + SM, [dts.b])
                TT("dve", adt, dtv, aneg[:], ALU.mult, [dts.b] + SM, [dts.b])
                MM([(pd[:, 16:32], onesf[:], adt, True, True)], [dts.b] + CONST, [Bpd])
                MM([(pd[:, 32:48], trilef[:], adt, True, True)], [dts.b] + CONST, [Bpd])
                MM([(pd[:, 48:64], sgtf[:], adt, True, True)], [dts.b] + CONST, [Bpd])
                ex = exring.next()
                ACT(ex.t[:, 0:48], pd[:, 16:64], AF.Exp, [Bpd], [ex.b])
                cd, eacs, dte = ex.t[:, 0:16], ex.t[:, 16:32], ex.t[:, 32:48]
                TT("dve", w1, dtv, dte, ALU.mult, [dts.b, ex.b], [dts.b])
                xdtd = xdtdring.next()
                TT("pool", xdtd.t[:].rearrange("p (h q) -> p h q", h=16), xt.t[:].rearrange("p (h q) -> p h q", h=16),
                   w1.unsqueeze(2).to_broadcast([128, 16, 64]), ALU.mult, [xt.b, dts.b], [xdtd.b])
                psts = []
                for g2 in range(2):
                    pst, Bpst = pr2.next()
                    MM([(pst[:, :], bt.t[:, g2 * 128:(g2 + 1) * 128], xdtd.t[:, g2 * 512:(g2 + 1) * 512], True, True)],
                       [bt.b, xdtd.b], [Bpst])
                    psts.append((pst, Bpst))
                TT("dve", state[:].rearrange("p (h q) -> p h q", h=16), state[:].rearrange("p (h q) -> p h q", h=16),
                   cd.unsqueeze(2).to_broadcast([128, 16, 64]), ALU.mult, [Bstate, ex.b], [Bstate])
                for g2 in range(2):
                    sl_ = slice(g2 * 512, (g2 + 1) * 512)
                    TT("dve", state[:, sl_], state[:, sl_], psts[g2][0][:, :], ALU.add, [Bstate, psts[g2][1]], [Bstate])
                nxt = m + 1
                if nxt < NCH and nxt >= 2 and nxt % 2 == 0:
                    sbf_cur = sbfring.next()
                    CP("pool", sbf_cur.t[:], state[:], [Bstate], [sbf_cur.b])
                if own:
                    jo = m // 2 - 1
                    gz = gzring.next()
                    gs = gsring.next()
                    for ng in range(4):
                        pz, Bpz = pr.next()
                        MM([(pz[:, :], ht.t[:, k, :], WB[:, k, ZS + ng * 512: ZS + (ng + 1) * 512], k == 0, k == 7)
                            for k in range(8)], [ht.b] + BWB, [Bpz])
                        if ng < 2:
                            ACT(gz.t[:, ng * 512:(ng + 1) * 512], pz[:, :], AF.Silu, [Bpz], [gz.b])
                        else:
                            ACT(gs.t[:, (ng - 2) * 512:(ng - 1) * 512], pz[:, :], AF.Silu, [Bpz], [gs.b])
                    DMA(GG[:, :, jo, :].rearrange("h t e -> t h e"), gs.t[:].rearrange("p (h e) -> p h e", h=8),
                        [gs.b], [], gs.sem)
                    xdt = xdtring.next()
                    TT("pool", xdt.t[:].rearrange("p (h q) -> p h q", h=16), xt.t[:].rearrange("p (h q) -> p h q", h=16),
                       dtv.unsqueeze(2).to_broadcast([128, 16, 64]), ALU.mult, [xt.b, dts.b], [xdt.b])
                    pcb, Bpcb = pr2.next()
                    for g2 in range(2):
                        MM([(pcb[:, g2 * 128:(g2 + 1) * 128], bct.t[:, g2, :], bct.t[:, 2 + g2, :], True, True)],
                           [bct.b], [Bpcb])
                    cbm = cbmring.next()
                    TT("dve", cbm.t[:], pcb[:, 0:256].rearrange("p (a b) -> p a b", a=2),
                       trilef[:].unsqueeze(1).to_broadcast([128, 2, 128]), ALU.mult, [Bpcb] + CONST, [cbm.b])
                    mt = mtring.next()
                    for b4 in range(4):
                        a4 = a4ring.next()
                        for hh in range(4):
                            hcol = adt[:, b4 * 4 + hh: b4 * 4 + hh + 1]
                            if hh % 2 == 0:
                                TSM("dve", a4.t[:, hh, :], sgtf[:], hcol, [dts.b] + CONST, [a4.b])
                            else:
                                ACT(a4.t[:, hh, :], sgtf[:], AF.Copy, [dts.b] + CONST, [a4.b], scale=hcol)
                        pe_, Bpe = pr2.next()
                        for hh in range(4):
                            MM([(pe_[:, hh * 128:(hh + 1) * 128], a4.t[:, hh, :], triler[:], True, True)],
                               [a4.b] + SM, [Bpe])
                        seg = segring.next()
                        ACT(seg.t[:].rearrange("p a b -> p (a b)"), pe_[:, :], AF.Exp, [Bpe], [seg.b])
                        TT("dve", mt.t[:, b4 * 4:(b4 + 1) * 4, :], seg.t[:],
                           cbm.t[:, b4 // 2, :].unsqueeze(1).to_broadcast([128, 4, 128]), ALU.mult,
                           [seg.b, cbm.b], [mt.b])
                    pys = []
                    for half in range(2):
                        py, Bpy = pr2.next()
                        for hh in range(8):
                            h = half * 8 + hh
                            MM([(py[:, hh * 64:(hh + 1) * 64], mt.t[:, h, :], xdt.t[:, h * 64:(h + 1) * 64], True, True)],
                               [mt.b, xdt.b], [Bpy])
                        pys.append((py, Bpy))
                    pos = []
                    for g2 in range(2):
                        po, Bpo = pr2.next()
                        MM([(po[:, :], bct.t[:, 2 + g2, :], sbf_cur.t[:, g2 * 512:(g2 + 1) * 512], True, True)],
                           [bct.b, sbf_cur.b], [Bpo])
                        pos.append((po, Bpo))
                    yo = yoring.next()
                    yy = yring.next()
                    y2 = y2ring.next()
                    for g2 in range(2):
                        sl_ = slice(g2 * 512, (g2 + 1) * 512)
                        TT("dve", yo.t[:, sl_].rearrange("p (h q) -> p h q", h=8),
                           pos[g2][0][:, :].rearrange("p (h q) -> p h q", h=8),
                           eacs[:, g2 * 8:(g2 + 1) * 8].unsqueeze(2).to_broadcast([128, 8, 64]), ALU.mult,
                           [pos[g2][1], ex.b], [yo.b])
                        TT("dve", yy.t[:, sl_], pys[g2][0][:, :], yo.t[:, sl_], ALU.add, [pys[g2][1], yo.b], [yy.b])
                    TT("pool", y2.t[:].rearrange("p (h q) -> p h q", h=16), xt.t[:].rearrange("p (h q) -> p h q", h=16),
                       dsk.unsqueeze(2).to_broadcast([128, 16, 64]), ALU.mult, [xt.b] + SM, [y2.b])
                    TT("pool", yy.t[:], yy.t[:], y2.t[:], ALU.add, [yy.b, y2.b], [yy.b])
                    TT("dve", yy.t[:], yy.t[:], gz.t[:], ALU.mult, [yy.b, gz.b], [yy.b])
                    for g2 in range(2):
                        ACT(junkb[:], yy.t[:, g2 * 512:(g2 + 1) * 512], AF.Square, [yy.b], [Bjunkb, Bst2a],
                            accum_out=st2[:, g2:g2 + 1])
                    RSQRT(st2[:, 0:2], st2[:, 2:4], st2[:, 4:6], 512, Bst2a, Bst2b, Bst2c)
                    ys = ysring.next()
                    for g2 in range(2):
                        sl_ = slice(g2 * 512, (g2 + 1) * 512)
                        STT("dve" if g2 == 0 else "pool", ys.t[:, sl_], yy.t[:, sl_], st2[:, 4 + g2:5 + g2], ssdg[:, sl_],
                            ALU.mult, ALU.mult, [yy.b, Bst2c] + SM, [ys.b])
                    DMA(YS[jo], ys.t[:], [ys.b], [], ys.sem)
            S.barrier()
            S.emit()
        pab.close()

        with ExitStack() as pcx:
            S.lat_pad = 0.0
            smallc = sb(pcx, "smallC", [128, 128 + 256 + 8], F32)
            gnb = smallc[:, 0:128]
            lamv = smallc[:, 128:384]
            lamt = smallc[:, 384:392]
            Bsc = S.buf("smallC")
            scsem = S.dma_sem("smallC")
            DMAS([(gnb, attg_d), (lamv, lamv_d)], [], [Bsc], scsem)
            Bsc2 = S.buf("smallC2")
            S.op("dve", lambda e: e.tensor_scalar_mul(out=gnb, in0=gnb, scalar1=0.8), [Bsc], [Bsc])
            lprod = sb(pcx, "lprod", [128, 128], F32)
            TT("dve", lprod[:, 0:64], lamv[:, 0:64], lamv[:, 64:128], ALU.mult, [Bsc], [Bsc2])
            TT("dve", lprod[:, 64:128], lamv[:, 128:192], lamv[:, 192:256], ALU.mult, [Bsc], [Bsc2])
            junkc = sb(pcx, "junkC", [128, 128], F32)
            ACT(junkc[:, 0:64], lprod[:, 0:64], AF.Copy, [Bsc2], [Bsc2], accum_out=lamt[:, 0:1])
            ACT(junkc[:, 64:128], lprod[:, 64:128], AF.Copy, [Bsc2], [Bsc2], accum_out=lamt[:, 1:2])
            ACT(lamt[:, 2:4], lamt[:, 0:2], AF.Exp, [Bsc2], [Bsc2])
            TT("dve", lamt[:, 4:5], lamt[:, 3:4], lamt[:, 2:3], ALU.subtract, [Bsc2], [Bsc2])
            S.op("dve", lambda e: e.tensor_scalar_add(out=lamt[:, 5:6], in0=lamt[:, 4:5], scalar1=-0.2), [Bsc2], [Bsc2])
            lamneg = lamt[:, 5:6]
            SC = [Bsc, Bsc2]

            KTb = [[sb(pcx, "KTb%d%d" % (p, j), [66, T], BF16) for j in range(2)] for p in range(2)]
            Vb = [sb(pcx, "Vb%d" % p, [128, NCH, 129], BF16) for p in range(2)]
            QTb = [[sb(pcx, "QTb%d%d" % (p, j), [66, TO], BF16) for j in range(2)] for p in range(2)]
            Gb = [sb(pcx, "Gb%d" % p, [128, NOWN, 128], BF16) for p in range(2)]
            KBb = [sb(pcx, "KBb%d" % p, [128, NG * NCH], F32) for p in range(2)]
            Bset = [S.buf("set%d" % p) for p in range(2)]
            setsem = [S.dma_sem("set%d" % p) for p in range(2)]
            BsetV = [S.buf("setV%d" % p) for p in range(2)]
            setsemV = [S.dma_sem("setV%d" % p) for p in range(2)]
            ptring = mkring(pcx, "pt", 4, [128, 2, 512], BF16)
            osb = sb(pcx, "osb", [128, 9, 160], F32)
            Bosb = S.buf("osb")
            fin = sb(pcx, "fin", [128, 32], F32)
            Bfin = [S.buf() for _ in range(5)]
            tmpo = sb(pcx, "tmpo", [128, 4, 128], F32)
            oo = sb(pcx, "oo", [128, 4, 128], F32)
            o2 = sb(pcx, "o2", [128, 4, 128], F32)
            Btmpo, Boo, Bo2 = S.buf(), S.buf(), S.buf()
            yaring = mkring(pcx, "yas", 2, [128, 4, 128], BF16, dma=True)
            spairs = [(2, 4, 5), (3, 6, 7)]
            spi = [0]

            def oacc(i, j):
                idx = i * 2 + j
                return psb[idx // 3][:, (idx % 3) * 160:(idx % 3) * 160 + 129], idx // 3

            def load_head(h):
                p = h % 2
                pairs = []
                for j in range(2):
                    pairs.append((KTb[p][j][0:64, :], KT[h, j * 64:(j + 1) * 64, :]))
                    pairs.append((QTb[p][j][0:64, :], QT[h, j * 64:(j + 1) * 64, :]))
                    pairs.append((QTb[p][j][64:66, :], qaug_d[h]))
                    pairs.append((KTb[p][j][64:66, :], ones_d))
                pairs.append((KBb[p][:], kbias_d[:, h * NG * NCH:(h + 1) * NG * NCH]))
                DMAS(pairs, [], [Bset[p]], setsem[p])
                DMAS([(Vb[p][:], VV[h]), (Gb[p][:], GG[h])], [], [BsetV[p]], setsemV[p])

            load_head(0)
            for h in range(8):
                p = h % 2
                if h + 1 < 8:
                    load_head(h + 1)
                for g in range(NG):
                    m0 = 8 * g + 2
                    steps = []
                    for c in range(m0 + 7):
                        i0 = 0 if c <= m0 else (c - m0 + 1) // 2
                        steps.append((c, i0))
                    LA = 2
                    pend = []
                    started = set()
                    for idx in range(len(steps) + LA):
                        if idx < len(steps):
                            c, i0 = steps[idx]
                            ncol = (4 - i0) * 128
                            big, b0, b1 = spairs[spi[0] % 2]
                            spi[0] += 1
                            MM([(psb[b0][:, 0:ncol] if j == 0 else psb[b1][:, 0:ncol],
                                 KTb[p][j][0:66, c * 128:(c + 1) * 128],
                                 QTb[p][j][0:66, g * 512 + i0 * 128:(g + 1) * 512], True, True) for j in range(2)],
                               [Bset[p]], [Bps[b0], Bps[b1]])
                            pt = ptring.next()
                            ACT(pt.t[:, :, 0:ncol], psbig[big][:, :].rearrange("p (j c) -> p j c", j=2)[:, :, 0:ncol],
                                AF.Exp, [Bps[b0], Bps[b1], Bset[p]], [pt.b], scale=0.125,
                                bias=KBb[p][:, g * NCH + c: g * NCH + c + 1])
                            if c >= m0 and (c - m0) % 2 == 0:
                                TT("pool", pt.t[:, :, 0:128], pt.t[:, :, 0:128],
                                   trileb[:].unsqueeze(1).to_broadcast([128, 2, 128]), ALU.mult, [pt.b] + CONST, [pt.b])
                            pend.append((c, i0, pt))
                        if idx >= LA:
                            c, i0, pt = pend[idx - LA]
                            mms = []
                            wb = set()
                            for j in range(2):
                                for ii in range(4 - i0):
                                    i = i0 + ii
                                    oap, bk = oacc(i, j)
                                    first = bk not in started
                                    started.add(bk)
                                    mms.append((oap, pt.t[:, j, ii * 128:(ii + 1) * 128], Vb[p][:, c, 0:129], first, True, True))
                                    wb.add(Bps[bk])
                            MM(mms, [pt.b, BsetV[p]], list(wb))
                    for bk in range(3):
                        CP("dve", osb[:, bk * 3:(bk + 1) * 3, :].rearrange("p a b -> p (a b)"), psb[bk][:, 0:480],
                           [Bps[bk]], [Bosb])
                    rl = fin[:, 0:8]
                    rln = fin[:, 8:12]
                    ssq = fin[:, 12:16]
                    lnv = fin[:, 16:20]
                    rs = fin[:, 20:24]
                    S.op("dve", lambda e: e.reciprocal(out=rl.unsqueeze(2), in_=osb[:, 0:8, 128:129]), [Bosb], [Bfin[0]])
                    TSM("dve", rln, rl.rearrange("p (a b) -> p a b", b=2)[:, :, 1], lamneg, [Bfin[0]] + SC, [Bfin[1]])
                    for i in range(4):
                        TSM("dve", tmpo[:, i, :], osb[:, 2 * i, 0:128], rl[:, 2 * i:2 * i + 1], [Bosb, Bfin[0]], [Btmpo])
                        STT("dve", oo[:, i, :], osb[:, 2 * i + 1, 0:128], rln[:, i:i + 1], tmpo[:, i, :], ALU.mult, ALU.add,
                            [Bosb, Bfin[1], Btmpo], [Boo])
                    TT("dve", tmpo[:], oo[:], oo[:], ALU.mult, [Boo], [Btmpo])
                    S.op("dve", lambda e: e.reduce_sum(out=ssq, in_=tmpo[:], axis=mybir.AxisListType.X), [Btmpo], [Bfin[2]], dur=0.6, lat=6.0)
                    RSQRT(ssq, lnv, rs, 128, Bfin[2], Bfin[3], Bfin[4])
                    for i in range(4):
                        STT("dve", o2[:, i, :], oo[:, i, :], rs[:, i:i + 1], gnb, ALU.mult, ALU.mult,
                            [Boo, Bfin[4]] + SC, [Bo2])
                    ya = yaring.next()
                    TT("pool", ya.t[:], o2[:], Gb[p][:, 4 * g:4 * g + 4, :], ALU.mult, [Bo2, BsetV[p]], [ya.b])
                    DMA(YA[4 * g:4 * g + 4, :, h * 128:(h + 1) * 128].rearrange("i t e -> t i e"), ya.t[:], [ya.b], [],
                        ya.sem)
            S.barrier()
            S.emit()

        with ExitStack() as pd_:
            WO = sb(pd_, "WO", [128, 16, 1024], BF16)
            BWO = [S.buf("WO%d" % k) for k in range(16)]
            wosem = S.dma_sem("woload")
            for q4 in range(4):
                DMAS([(WO[:, kt, :], WOS[kt]) for kt in range(q4 * 4, q4 * 4 + 4)], [], BWO[q4 * 4:q4 * 4 + 4], wosem)
            fng = sb(pd_, "fng", [128, 1024], F32)
            Bfng = S.buf("fng")
            fsem = S.dma_sem("fng")
            DMA(fng[:], fng_d, [], [Bfng], fsem)
            ysl = mkring(pd_, "ysl", 3, [128, 2048], BF16, dma=True)
            xol = mkring(pd_, "xol", 3, [128, 1024], F32, dma=True)
            ytr = mkring(pd_, "ytr", 3, [128, 16, 128], BF16)
            hs = mkring(pd_, "hs", 2, [128, 1024], F32)
            outr = mkring(pd_, "outr", 2, [128, 1024], F32, dma=True)
            junkd = sb(pd_, "junkD", [128, 1024], BF16)
            Bjunkd = S.buf()
            st3 = sb(pd_, "stD", [128, 4], F32)
            Bs3 = [S.buf() for _ in range(3)]
            pr = PsRing(list(range(8)))
            def prepD(jo):
                m = 2 * jo + 2
                yl = ysl.next()
                DMAS([(yl.t[:, 0:1024], YS[jo]), (yl.t[:, 1024:2048], YA[jo])], [], [yl.b], yl.sem)
                xo = xol.next()
                DMA(xo.t[:], xl[m * 128:(m + 1) * 128, :], [], [xo.b], xo.sem)
                yt = ytr.next()
                for half in range(2):
                    pt_, Bpt_ = pr.next()
                    ptb = pt_.bitcast(BF16)
                    TR([(ptb[:, k * 128:(k + 1) * 128], yl.t[:, (half * 8 + k) * 128:(half * 8 + k + 1) * 128], identb[:])
                        for k in range(8)], [yl.b] + CONST, [Bpt_])
                    CP("act" if half == 0 else "dve", yt.t[:, half * 8:(half + 1) * 8, :].rearrange("p a b -> p (a b)"),
                       ptb[:, 0:1024], [Bpt_], [yt.b])
                return yt, xo

            def mainD(jo, yt, xo):
                hsum = hs.next()
                for ng in range(2):
                    po, Bpo = pr.next()
                    MM([(po[:, :], yt.t[:, k, :], WO[:, k, ng * 512:(ng + 1) * 512], k == 0, k == 15) for k in range(16)],
                       [yt.b] + BWO, [Bpo])
                    TT("dve", hsum.t[:, ng * 512:(ng + 1) * 512], po[:, :], xo.t[:, ng * 512:(ng + 1) * 512], ALU.add,
                       [Bpo, xo.b], [hsum.b])
                ACT(junkd[:], hsum.t[:], AF.Square, [hsum.b], [Bjunkd, Bs3[0]], accum_out=st3[:, 0:1])
                RSQRT(st3[:, 0:1], st3[:, 1:2], st3[:, 2:3], D, Bs3[0], Bs3[1], Bs3[2])
                ot = outr.next()
                STT("dve", ot.t[:, 0:512], hsum.t[:, 0:512], st3[:, 2:3], fng[:, 0:512], ALU.mult, ALU.mult,
                    [hsum.b, Bs3[2], Bfng], [ot.b])
                STT("dve", ot.t[:, 512:1024], hsum.t[:, 512:1024], st3[:, 2:3], fng[:, 512:1024], ALU.mult, ALU.mult,
                    [hsum.b, Bs3[2], Bfng], [ot.b])
                DMA(out_d[jo * 128:(jo + 1) * 128, :], ot.t[:], [ot.b], [], ot.sem)

            prepped = {}
            for jo in range(NOWN + 1):
                if jo < NOWN:
                    prepped[jo] = prepD(jo)
                if jo >= 1:
                    mainD(jo - 1, *prepped.pop(jo - 1))
            S.final_wait("sp")
            S.emit()
    return nc


def _bc(v, n=128):
    return np.ascontiguousarray(np.broadcast_to(np.asarray(v, np.float32).reshape(1, -1), (n, np.size(v))))


def prep(inputs, NXC):
    NCH = NXC + 1
    NOWN = NXC // 2
    NG = NOWN // 4
    f = lambda a: np.asarray(a, np.float32)
    x = f(inputs["x"])
    meta = f(inputs["meta"])
    w_in = f(inputs["w_in"])[0]
    zs, xbc, dtc = w_in[:, 0:1024], w_in[:, 1024:2560], w_in[:, 2560:2576]
    q, k, v, za = w_in[:, 2576:3600], w_in[:, 3600:4624], w_in[:, 4624:5648], w_in[:, 5648:6672]
    wA = np.ascontiguousarray(np.concatenate([k, v, q], axis=1))
    wB = np.ascontiguousarray(np.concatenate([xbc, dtc, zs, za], axis=1))
    wo = np.ascontiguousarray(f(inputs["w_out"])[0])
    gbc = _bc(f(inputs["norm_g"])[0])
    cwf = f(inputs["conv_w"])[0]
    cw = np.ascontiguousarray(cwf.reshape(4, 12, 128).transpose(2, 1, 0).reshape(128, 48))
    cb = f(inputs["conv_b"])[0]
    cbbc = _bc(cb[0:1280])
    cbcol = np.ascontiguousarray(cb.reshape(12, 128).T)
    vecs = np.concatenate([_bc(f(inputs["dt_bias"])[0]), _bc(f(inputs["a_log"])[0]), _bc(f(inputs["d_skip"])[0])], axis=1)
    ssdg = _bc(f(inputs["ssd_norm_g"])[0])
    attg = _bc(f(inputs["attn_norm_g"])[0])
    fng = _bc(f(inputs["final_norm_g"]))
    lamv = np.concatenate([_bc(f(inputs[n])[0]) for n in ("lambda_q1", "lambda_k1", "lambda_q2", "lambda_k2")], axis=1)
    ident = np.eye(128, dtype=np.float32)
    ar = np.arange(128)
    trile = (ar[:, None] <= ar[None, :]).astype(np.float32)
    sgt = (ar[:, None] > ar[None, :]).astype(np.float32)
    slopes = 2.0 ** (-8.0 * np.arange(1, 9) / 8.0)
    col = np.arange(512)
    qaug = np.zeros((8, 2, 512), np.float32)
    for h in range(8):
        qaug[h, 0] = -slopes[h] * 8.0 * (col % 128)
        qaug[h, 1] = -slopes[h] * 8.0 * 256.0 * (col // 128)
    import ml_dtypes
    qaug = np.ascontiguousarray(np.tile(qaug, (1, 1, NG)).astype(ml_dtypes.bfloat16))
    onesk = np.ones((2, NCH * 128), dtype=ml_dtypes.bfloat16)
    shared = dict(onesk=onesk, wA=wA, wB=wB, wo=wo, gbc=gbc, cw=cw, cbbc=cbbc, cbcol=cbcol, vecs=np.ascontiguousarray(vecs),
                  ssdg=ssdg, attg=attg, fng=fng, lamv=np.ascontiguousarray(lamv), ident=ident, trile=trile, sgt=sgt,
                  qaug=qaug)
    in_maps = []
    B = x.shape[0]
    for b in range(B):
        for r in range(2):
            if r == 1:
                xl = np.concatenate([np.zeros((N_PAD, D), np.float32), meta, x[b, 0:NXC * 128]], axis=0)
                nvalid0 = N_PAD
            else:
                xl = np.concatenate([np.zeros((128 + N_PAD, D), np.float32), meta, x[b, 0:(NXC - 1) * 128]], axis=0)
                nvalid0 = 128 + N_PAD
            pos = np.arange(NCH * 128)
            val = (pos >= nvalid0).astype(np.float32)
            valid = np.ascontiguousarray(val.reshape(NCH, 128).T)
            kb = np.zeros((128, 8, NG, NCH), np.float32)
            for g in range(NG):
                m0 = 8 * g + 2
                for c in range(NCH):
                    base = ar[:, None] - 128.0 * (m0 - c)
                    kb[:, :, g, c] = base * slopes[None, :] + np.where(val[c * 128 + ar] > 0, 0.0, NEGBIG)[:, None]
            d = dict(shared)
            d["xl"] = np.ascontiguousarray(xl)
            d["valid"] = valid
            d["kbias"] = np.ascontiguousarray(kb.reshape(128, 8 * NG * NCH))
            in_maps.append(d)
    return in_maps


def assemble(results, B, NXC):
    NOWN = NXC // 2
    out = np.zeros((B, NXC * 128, D), np.float32)
    for b in range(B):
        for r in range(2):
            o = results[b * 2 + r]["out"]
            for jo in range(NOWN):
                xc = 2 * jo + r
                out[b, xc * 128:(xc + 1) * 128] = o[jo * 128:(jo + 1) * 128]
    return out


def kernel(**inputs):
    x = np.asarray(inputs["x"])
    B, SEQ, _ = x.shape
    NXC = SEQ // 128
    nc = build(NXC)
    in_maps = prep(inputs, NXC)
    res = run_bass_kernel_spmd(nc, in_maps, core_ids=list(range(2 * B)))
    return assemble(res.results, B, NXC)
```

```python
import math
from contextlib import ExitStack

import numpy as np
import concourse.bass as bass
import concourse.mybir as mybir
from concourse.bass_utils import run_bass_kernel_spmd

F32 = mybir.dt.float32
BF16 = mybir.dt.bfloat16
F32R = mybir.dt.float32r
AF = mybir.ActivationFunctionType
ALU = mybir.AluOpType

EPS = 1e-5
LATPAD_B = 0.0
NEGBIG = -30000.0
D = 1024
N_META = 16
N_PAD = 112


class Buf:
    __slots__ = ("name", "w", "r")

    def __init__(self, name):
        self.name = name
        self.w = None
        self.r = []


class Slot:
    def __init__(self, t, b, sem):
        self.t, self.b, self.sem = t, b, sem


class Ring:
    def __init__(self, slots):
        self.slots = slots
        self.i = 0

    def next(self):
        s = self.slots[self.i % len(self.slots)]
        self.i += 1
        return s


class Node:
    __slots__ = ("idx", "eng", "fn", "deps", "dur", "lat", "dsem", "ndma", "tok", "succ", "nd", "ready", "done", "tbl")

    def __init__(self, idx, eng, fn, deps, dur, lat, dsem, ndma):
        self.idx, self.eng, self.fn, self.deps = idx, eng, fn, deps
        self.dur, self.lat, self.dsem, self.ndma = dur, lat, dsem, ndma
        self.tok = None
        self.succ = []
        self.nd = 0
        self.ready = 0.0
        self.done = False
        self.tbl = None


class Sched:
    ENG = ("pe", "act", "dve", "pool", "sp")

    def __init__(self, nc, es):
        self.nc = nc
        self.es = es
        self.sems = {}
        self.cnt = {}
        for e in ("pe", "act", "dve", "pool"):
            self.sems[e] = es.enter_context(nc.semaphore("s_" + e))
            self.cnt[e] = 0
        self.nodes = []
        self.seen = {e: {} for e in self.ENG}
        self.pending = {e: [] for e in self.ENG}
        self.nidx = 0
        self.nbuf = 0
        self.final = None
        self.lat_pad = 0.0

    def buf(self, name=None):
        self.nbuf += 1
        return Buf(name or "b%d" % self.nbuf)

    def dma_sem(self, name):
        key = "d_" + name
        assert key not in self.sems
        self.sems[key] = self.es.enter_context(self.nc.semaphore(key))
        self.cnt[key] = 0
        return key

    def op(self, eng, fn, reads=(), writes=(), dsem=None, ndma=1, dur=0.3, lat=0.15, tbl=None):
        deps = set()
        for b in reads:
            if b.w is not None:
                deps.add(b.w)
        for b in writes:
            if b.w is not None:
                deps.add(b.w)
            deps.update(b.r)
        self.nidx += 1
        n = Node(self.nidx, eng, fn, deps, dur, lat + (self.lat_pad if dsem is None else 0.0), dsem, ndma)
        n.tbl = tbl
        for b in writes:
            b.w = n
            b.r = []
        for b in reads:
            b.r.append(n)
        self.nodes.append(n)
        return n

    def barrier(self):
        self._barrier = True

    def final_wait(self, eng="sp"):
        self.final = eng

    def _schedule(self, nodes):
        import heapq
        for n in nodes:
            n.nd = 0
            n.ready = 0.0
        for n in nodes:
            for d in n.deps:
                if not d.done:
                    d.succ.append(n)
                    n.nd += 1
        A = {e: [] for e in self.ENG}
        Bq = {e: [] for e in self.ENG}
        tm = {e: 0.0 for e in self.ENG}
        order = {e: [] for e in self.ENG}
        cur_tbl = [0]
        for n in nodes:
            if n.nd == 0:
                heapq.heappush(A[n.eng], (0.0, n.idx, n))
        left = len(nodes)

        def act_pick(tm_e):
            a, bq = A["act"], Bq["act"]
            opts = []
            for cnd in heapq.nsmallest(12, bq):
                sw = cnd[1].tbl not in (None, cur_tbl[0])
                opts.append((tm_e + (1.3 if sw else 0.0), cnd[0], cnd[1], True))
            for cnd in heapq.nsmallest(6, a):
                sw = cnd[2].tbl not in (None, cur_tbl[0])
                opts.append((max(tm_e, cnd[0]) + (1.3 if sw else 0.0), cnd[1], cnd[2], False))
            if not opts:
                return None
            return min(opts, key=lambda o: (o[0], o[1]))

        while left:
            best = None
            for e in self.ENG:
                a, bq = A[e], Bq[e]
                while a and a[0][0] <= tm[e]:
                    r, i, n = heapq.heappop(a)
                    heapq.heappush(bq, (i, n))
                if e == "act":
                    o = act_pick(tm[e])
                    if o is None:
                        continue
                    cand = (o[0], o[1], e, o)
                elif bq:
                    cand = (tm[e], bq[0][0], e, True)
                elif a:
                    cand = (a[0][0], a[0][1], e, False)
                else:
                    continue
                if best is None or cand[:2] < best[:2]:
                    best = cand
            start, _, e, fromb = best
            if e == "act":
                o = fromb
                n = o[2]
                if o[3]:
                    Bq[e].remove((n.idx, n))
                    heapq.heapify(Bq[e])
                else:
                    A[e].remove((n.ready, n.idx, n))
                    heapq.heapify(A[e])
                if n.tbl is not None:
                    cur_tbl[0] = n.tbl
            elif fromb:
                _, n = heapq.heappop(Bq[e])
            else:
                _, _, n = heapq.heappop(A[e])
            fin = start + n.dur
            tm[e] = fin
            avail = fin + n.lat
            order[e].append(n)
            n.done = True
            left -= 1
            for s_ in n.succ:
                if s_.ready < avail:
                    s_.ready = avail
                s_.nd -= 1
                if s_.nd == 0:
                    heapq.heappush(A[s_.eng], (s_.ready, s_.idx, s_))
            n.succ = []
        self.est = max(tm.values())
        self.est_busy = {e: sum(n.dur for n in order[e]) for e in self.ENG}
        return order

    def emit(self):
        nc = self.nc
        sems = self.sems
        nodes = self.nodes
        self.nodes = []
        order = self._schedule(nodes)
        for e in self.ENG:
            for n in order[e]:
                if n.dsem is not None:
                    self.cnt[n.dsem] += 16 * n.ndma
                    n.tok = (n.dsem, self.cnt[n.dsem])
                else:
                    self.cnt[e] += 1
                    n.tok = (e, self.cnt[e])
        prog = {}
        for e in self.ENG:
            lst = []
            seen = self.seen[e]
            first = True
            for n in order[e]:
                waits = {}
                toks = [d.tok for d in n.deps]
                if first:
                    toks += self.pending[e]
                    self.pending[e] = []
                    first = False
                for (k, v) in toks:
                    if e == "pe" and k == "pe":
                        continue
                    if seen.get(k, 0) >= v:
                        continue
                    if waits.get(k, 0) < v:
                        waits[k] = v
                for k, v in waits.items():
                    seen[k] = v
                inc = (n.dsem, 16) if n.dsem is not None else (e, 1)
                lst.append((list(waits.items()), n.fn, inc))
                n.deps = None
                n.fn = None
            prog[e] = lst
        if getattr(self, "_barrier", False):
            toks = [(k, v) for k, v in self.cnt.items() if v > 0]
            for e in self.ENG:
                self.pending[e].extend(toks)
            self._barrier = False
        if self.final is not None:
            toks = [(k, v) for k, v in self.cnt.items() if v > 0]
            prog[self.final].append((toks, None, None))
            self.final = None

        def run(e, name):
            for waits, fn, inc in prog[name]:
                for k, v in waits:
                    e.wait_ge(sems[k], v)
                if fn is None:
                    continue
                r = fn(e)
                if isinstance(r, (list, tuple)):
                    for ins in r:
                        ins.then_inc(sems[inc[0]], inc[1])
                else:
                    r.then_inc(sems[inc[0]], inc[1])

        with nc.Block() as block:
            @block.tensor
            def _(e):
                run(e, "pe")

            @block.scalar
            def _(e):
                run(e, "act")

            @block.vector
            def _(e):
                run(e, "dve")

            @block.gpsimd
            def _(e):
                run(e, "pool")

            @block.sync
            def _(e):
                run(e, "sp")


def build(NXC, dbg=False):
    NCH = NXC + 1
    NOWN = NXC // 2
    NG = NOWN // 4
    assert NOWN % 4 == 0
    T = NCH * 128
    TO = NOWN * 128
    nc = bass.Bass("TRN2", target_bir_lowering=False)

    def din(name, shape, dt=F32):
        return nc.dram_tensor(name, list(shape), dt, kind="ExternalInput").ap()

    skind = "ExternalOutput" if dbg else "Internal"

    def dscr(name, shape, dt=BF16):
        return nc.dram_tensor(name, list(shape), dt, kind=skind).ap()

    xl = din("xl", [T, D])
    valid_d = din("valid", [128, NCH])
    wA_d = din("wA", [D, 3072])
    wB_d = din("wB", [D, 3600])
    wo_d = din("wo", [2048, D])
    gbc_d = din("gbc", [128, D])
    cw_d = din("cw", [128, 48])
    cbbc_d = din("cbbc", [128, 1280])
    cbcol_d = din("cbcol", [128, 12])
    vecs_d = din("vecs", [128, 48])
    ssdg_d = din("ssdg", [128, 1024])
    attg_d = din("attg", [128, 128])
    fng_d = din("fng", [128, 1024])
    lamv_d = din("lamv", [128, 256])
    ident_d = din("ident", [128, 128])
    trile_d = din("trile", [128, 128])
    sgt_d = din("sgt", [128, 128])
    qaug_d = din("qaug", [8, 2, NG * 512], BF16)
    kbias_d = din("kbias", [128, 8 * NG * NCH])
    ones_d = din("onesk", [2, T], BF16)
    out_d = nc.dram_tensor("out", [TO, D], F32, kind="ExternalOutput").ap()

    HNT = dscr("HNT", [NCH, 128, 1024])
    KT = dscr("KT", [8, 128, T])
    VV = dscr("VV", [8, 128, NCH, 129])
    QT = dscr("QT", [8, 128, TO])
    GG = dscr("GG", [8, 128, NOWN, 128])
    YS = dscr("YS", [NOWN, 128, 1024])
    YA = dscr("YA", [NOWN, 128, 1024])
    WOS = dscr("WOS", [16, 128, 1024])

    with ExitStack() as es:
        S = Sched(nc, es)

        def sb(stack, name, shape, dt):
            return stack.enter_context(nc.sbuf_tensor("s_" + name, list(shape), dt))

        def mkring(stack, name, n, shape, dt, dma=False):
            return Ring([Slot(sb(stack, "%s%d" % (name, i), shape, dt), S.buf("%s%d" % (name, i)),
                              S.dma_sem("%s%d" % (name, i)) if dma else None) for i in range(n)])

        def fsz(ap):
            n = 1
            for d in ap.shape[1:]:
                n *= d
            return n

        def nbytes(ap):
            n = ap.shape[0]
            for d in ap.shape[1:]:
                n *= d
            return n * (2 if ap.dtype == BF16 else 4)

        def DMA(out, in_, reads, writes, dsem, eng="sp"):
            return S.op(eng, lambda e: e.dma_start(out=out, in_=in_), reads, writes, dsem=dsem,
                        dur=0.5, lat=2.0 + nbytes(out) / 120e3)

        def DMAS(pairs, reads, writes, dsem, eng="sp"):
            return S.op(eng, lambda e: [e.dma_start(out=o, in_=i) for (o, i) in pairs], reads, writes,
                        dsem=dsem, ndma=len(pairs), dur=0.5 * len(pairs),
                        lat=2.0 + sum(nbytes(o) for o, _ in pairs) / 120e3)

        def ACT(out, in_, func, reads, writes, **kw):
            tbl = 1 if func == AF.Silu else (0 if func in (AF.Exp, AF.Ln) else None)
            return S.op("act", lambda e: e.activation(out=out, in_=in_, func=func, **kw), reads, writes,
                        dur=0.2 + 0.00083 * fsz(in_), tbl=tbl)

        def vdur(eng, ap):
            return (0.1 + 0.0015 * fsz(ap)) if eng == "dve" else (0.15 + 0.0036 * fsz(ap))

        def CP(eng, out, in_, reads, writes):
            if eng == "act":
                return ACT(out, in_, AF.Copy, reads, writes)
            return S.op(eng, lambda e: e.tensor_copy(out=out, in_=in_), reads, writes, dur=vdur(eng, out))

        def TT(eng, out, in0, in1, op, reads, writes):
            return S.op(eng, lambda e: e.tensor_tensor(out=out, in0=in0, in1=in1, op=op), reads, writes,
                        dur=vdur(eng, out))

        def TSM(eng, out, in0, scalar1, reads, writes):
            eng = "dve"
            return S.op(eng, lambda e: e.tensor_scalar_mul(out=out, in0=in0, scalar1=scalar1), reads, writes,
                        dur=vdur(eng, out))

        def STT(eng, out, in0, scalar, in1, op0, op1, reads, writes):
            eng = "dve"
            return S.op(eng, lambda e: e.scalar_tensor_tensor(out=out, in0=in0, scalar=scalar, in1=in1,
                                                              op0=op0, op1=op1), reads, writes, dur=vdur(eng, out))

        def MSET(eng, ap, val, writes):
            return S.op(eng, lambda e: e.memset(ap, val), (), writes, dur=vdur(eng, ap))

        def MM(mms, reads, writes):
            def fn(e):
                last = None
                for mm in mms:
                    if len(mm) == 6:
                        last = e.matmul(mm[0], mm[1], mm[2], start=mm[3], stop=mm[4], skip_group_check=True)
                    else:
                        last = e.matmul(mm[0], mm[1], mm[2], start=mm[3], stop=mm[4])
                return last
            d = 0.0
            for mm in mms:
                passes = 4 if mm[1].dtype == F32 else 1
                d += passes * max(0.058, fsz(mm[2]) / 2400.0 + 0.004)
            return S.op("pe", fn, reads, writes, dur=d, lat=0.25)

        def TR(items, reads, writes):
            def fn(e):
                last = None
                for (o, i, idn) in items:
                    last = e.transpose(o, i, idn)
                return last
            return S.op("pe", fn, reads, writes, dur=0.07 * len(items), lat=0.3)

        def RSQRT(ss, lnt, rs, n, Bss, Blnt, Brs):
            ACT(lnt, ss, AF.Ln, [Bss] + CONST, [Blnt], scale=1.0 / n, bias=epsb[:, 0:1])
            ACT(rs, lnt, AF.Exp, [Blnt], [Brs], scale=-0.5)

        psbig = [es.enter_context(nc.psum_tensor("psbig%d" % i, [128, 1024], F32)) for i in range(4)]
        psb = [psbig[i // 2][:, (i % 2) * 512:(i % 2 + 1) * 512] for i in range(8)]
        Bps = [S.buf("ps%d" % i) for i in range(8)]

        class PsRing:
            def __init__(self, banks):
                self.banks = banks
                self.i = 0

            def next(self):
                k = self.banks[self.i % len(self.banks)]
                self.i += 1
                return psb[k], Bps[k]

        identf = sb(es, "identf", [128, 128], F32)
        identb = sb(es, "identb", [128, 128], BF16)
        trilef = sb(es, "trilef", [128, 128], F32)
        trileb = sb(es, "trileb", [128, 128], BF16)
        sgtf = sb(es, "sgtf", [128, 128], F32)
        onesf = sb(es, "onesf", [128, 128], F32)
        onesrow = sb(es, "onesrow", [1, 128], BF16)
        epsb = sb(es, "epsb", [128, 1], F32)
        Bconst = S.buf("const")
        csem = S.dma_sem("const")
        DMAS([(identf[:], ident_d), (trilef[:], trile_d), (sgtf[:], sgt_d)], [], [Bconst], csem)
        Bc2 = S.buf("const2")
        CP("dve", identb[:], identf[:], [Bconst], [Bc2])
        CP("dve", trileb[:], trilef[:], [Bconst], [Bc2])
        MSET("dve", onesf[:], 1.0, [Bc2])
        MSET("dve", onesrow[:], 1.0, [Bc2])
        MSET("dve", epsb[:], EPS, [Bc2])
        CONST = [Bconst, Bc2]

        def load_weights(wst, dst, Bdst, src, ncols, nkt, engs=("dve", "pool", "act")):
            i = 0
            for kt in range(nkt):
                pw = wst.slots[0].t.shape[1]
                for c0 in range(0, ncols, pw):
                    w = min(pw, ncols - c0)
                    sl = wst.next()
                    DMA(sl.t[:, 0:w], src[kt * 128:(kt + 1) * 128, c0:c0 + w], [], [sl.b], sl.sem)
                    CP(engs[i % len(engs)], dst[:, kt, c0:c0 + w], sl.t[:, 0:w], [sl.b], [Bdst[kt]])
                    i += 1


        pab = es.enter_context(ExitStack())
        WB = sb(pab, "WB", [128, 8, 3600], BF16)
        BWB = [S.buf("WB%d" % k) for k in range(8)]
        with ExitStack() as pa:
            WA = sb(pa, "WA", [128, 8, 3072], BF16)
            BWA = [S.buf("WA%d" % k) for k in range(8)]
            load_weights(mkring(pa, "wstA", 5, [128, 1024], F32, dma=True), WA, BWA, wA_d, 3072, 8)
            gbc = sb(pa, "gbc", [128, D], F32)
            Bgbc = S.buf("gbc")
            gsem = S.dma_sem("gbc")
            DMA(gbc[:], gbc_d, [], [Bgbc], gsem)
            xring = mkring(pa, "xs", 3, [128, D], F32, dma=True)
            hnring = mkring(pa, "hn", 3, [128, D], BF16)
            hntring = mkring(pa, "hnT", 3, [128, 8, 128], BF16, dma=True)
            ktring = mkring(pa, "kts", 2, [128, 8, 128], BF16, dma=True)
            vring = mkring(pa, "vs", 2, [128, 8, 129], BF16, dma=True)
            qtring = mkring(pa, "qts", 2, [128, 8, 128], BF16, dma=True)
            junk = sb(pa, "junkA", [128, D], BF16)
            Bjunk = S.buf()
            st = sb(pa, "stA", [128, 4], F32)
            Bss, Blnt, Brs = S.buf(), S.buf(), S.buf()
            pr = PsRing(list(range(8)))
            for sl in vring.slots:
                MSET("pool", sl.t[:, :, 128:129], 1.0, [sl.b])

            def prepA(m):
                xs = xring.next()
                DMA(xs.t[:], xl[m * 128:(m + 1) * 128, :], [], [xs.b], xs.sem)
                ACT(junk[:], xs.t[:], AF.Square, [xs.b], [Bjunk, Bss], accum_out=st[:, 0:1])
                RSQRT(st[:, 0:1], st[:, 1:2], st[:, 2:3], D, Bss, Blnt, Brs)
                hn = hnring.next()
                STT("dve", hn.t[:], xs.t[:], st[:, 2:3], gbc[:], ALU.mult, ALU.mult, [xs.b, Brs, Bgbc], [hn.b])
                pt, Bpt = pr.next()
                ptb = pt.bitcast(BF16)
                TR([(ptb[:, k * 128:(k + 1) * 128], hn.t[:, k * 128:(k + 1) * 128], identb[:]) for k in range(8)],
                   [hn.b] + CONST, [Bpt])
                ht = hntring.next()
                CP("act", ht.t[:].rearrange("p a b -> p (a b)"), ptb[:, 0:1024], [Bpt], [ht.b])
                DMA(HNT[m], ht.t[:].rearrange("p a b -> p (a b)"), [ht.b], [], ht.sem)
                return ht

            def mainA(m, ht):
                def proj_fm(col0, ring_, dst_ap):
                    sl = ring_.next()
                    for half in range(2):
                        pk, Bpk = pr.next()
                        for hh in range(4):
                            h = half * 4 + hh
                            MM([(pk[:, hh * 128:(hh + 1) * 128],
                                 WA[:, k, col0 + h * 128: col0 + (h + 1) * 128], ht.t[:, k, :],
                                 k == 0, k == 7) for k in range(8)], [ht.b] + BWA, [Bpk])
                        CP("dve" if half == 0 else "act",
                           sl.t[:, half * 4:(half + 1) * 4, :].rearrange("p a b -> p (a b)"), pk[:, :], [Bpk], [sl.b])
                    DMA(dst_ap, sl.t[:], [sl.b], [], sl.sem)

                proj_fm(0, ktring, KT[:, :, m * 128:(m + 1) * 128].rearrange("h p t -> p h t"))
                vs = vring.next()
                for ng in range(2):
                    pv, Bpv = pr.next()
                    MM([(pv[:, :], ht.t[:, k, :], WA[:, k, 1024 + ng * 512: 1024 + (ng + 1) * 512], k == 0, k == 7)
                        for k in range(8)], [ht.b] + BWA, [Bpv])
                    CP("act" if ng == 0 else "dve", vs.t[:, ng * 4:(ng + 1) * 4, 0:128],
                       pv[:, :].rearrange("p (a b) -> p a b", a=4), [Bpv], [vs.b])
                DMA(VV[:, :, m, :].rearrange("h t e -> t h e"), vs.t[:], [vs.b], [], vs.sem)
                if m >= 2 and m % 2 == 0:
                    jo = m // 2 - 1
                    proj_fm(2048, qtring, QT[:, :, jo * 128:(jo + 1) * 128].rearrange("h p t -> p h t"))

            hts = {}
            for m in range(NCH + 1):
                if m < NCH:
                    hts[m] = prepA(m)
                if m >= 1:
                    mainA(m - 1, hts.pop(m - 1))
            load_weights(mkring(pa, "wstB", 2, [128, 1024], F32, dma=True), WB, BWB, wB_d, 3600, 8, engs=("pool",))
            wost = mkring(pa, "wost", 2, [128, 1024], F32, dma=True)
            wobf = mkring(pa, "wobf", 2, [128, 1024], BF16, dma=True)
            for kt in range(16):
                sl = wost.next()
                DMA(sl.t[:], wo_d[kt * 128:(kt + 1) * 128, :], [], [sl.b], sl.sem)
                ob = wobf.next()
                CP("pool", ob.t[:], sl.t[:], [sl.b], [ob.b])
                DMA(WOS[kt], ob.t[:], [ob.b], [], ob.sem)
            S.barrier()
            S.emit()

        with ExitStack() as pb:
            S.lat_pad = LATPAD_B
            XC, DTC, ZS, ZA = 0, 1536, 1552, 2576
            small = sb(pb, "smallB", [128, 48 + 12 + 48 + NCH], F32)
            cw = small[:, 0:48]
            cbcol = small[:, 48:60]
            vecs = small[:, 60:108]
            validt = small[:, 108:108 + NCH]
            cbbc = sb(pb, "cbbc", [128, 1280], F32)
            ssdg = sb(pb, "ssdg", [128, 1024], F32)
            Bsm = S.buf("smallB")
            smsem = S.dma_sem("smallB")
            DMAS([(cw, cw_d), (cbcol, cbcol_d), (vecs, vecs_d), (validt, valid_d), (cbbc[:], cbbc_d),
                  (ssdg[:], ssdg_d)], [], [Bsm], smsem)
            Bsm2 = S.buf("smallB2")
            dtb = vecs[:, 0:16]
            aneg = sb(pb, "aneg", [128, 16], F32)
            ACT(aneg[:], vecs[:, 16:32], AF.Exp, [Bsm], [Bsm2])
            S.op("dve", lambda e: e.tensor_scalar_mul(out=aneg[:], in0=aneg[:], scalar1=-1.0), [Bsm2], [Bsm2])
            dsk = vecs[:, 32:48]
            Wd = sb(pb, "Wd", [128, 48, 128], BF16)
            for i in range(48):
                TSM("pool" if i % 2 else "dve", Wd[:, i, :], identf[:], cw[:, i:i + 1], [Bsm] + CONST, [Bsm2])
            SM = [Bsm, Bsm2]

            hntring = mkring(pb, "hnTb", 3, [128, 8, 128], BF16, dma=True)
            xrring = mkring(pb, "xr", 2, [128, 12, 131], BF16)
            for sl in xrring.slots:
                MSET("pool", sl.t[:], 0.0, [sl.b])
            xtring = mkring(pb, "xtm", 3, [128, 1024], F32)
            cvring = mkring(pb, "cvt", 2, [128, 512], F32)
            btring = mkring(pb, "btm", 3, [128, 256], BF16)
            bctring = mkring(pb, "bct", 2, [128, 4, 128], BF16)
            dtring = mkring(pb, "dts", 3, [128, 96], F32)
            exring = mkring(pb, "exs", 3, [128, 48], F32)
            xdtdring = mkring(pb, "xdtd", 3, [128, 1024], BF16)
            xdtring = mkring(pb, "xdt", 2, [128, 1024], BF16)
            state = sb(pb, "state", [128, 1024], F32)
            Bstate = S.buf("state")
            MSET("dve", state[:], 0.0, [Bstate])
            sbfring = mkring(pb, "sbf", 2, [128, 1024], BF16)
            MSET("pool", sbfring.slots[0].t[:], 0.0, [sbfring.slots[0].b])
            gzring = mkring(pb, "gz", 2, [128, 1024], F32)
            gsring = mkring(pb, "gs", 2, [128, 1024], BF16, dma=True)
            a4ring = mkring(pb, "a4", 2, [128, 4, 128], F32R)
            triler = sb(pb, "triler", [128, 128], F32R)
            CP("dve", triler[:], trilef[:], CONST, [Bsm2])
            segring = mkring(pb, "seg", 2, [128, 4, 128], F32)
            mtring = mkring(pb, "mt", 2, [128, 16, 128], BF16)
            cbmring = mkring(pb, "cbm", 2, [128, 2, 128], F32)
            yoring = mkring(pb, "yo", 1, [128, 1024], F32)
            yring = mkring(pb, "yy", 1, [128, 1024], F32)
            y2ring = mkring(pb, "y2", 1, [128, 1024], F32)
            ysring = mkring(pb, "yss", 2, [128, 1024], BF16, dma=True)
            junkb = sb(pb, "junkB", [128, 512], BF16)
            Bjunkb = S.buf()
            st2 = sb(pb, "stB", [128, 8], F32)
            Bst2a, Bst2b, Bst2c = S.buf(), S.buf(), S.buf()
            pr = PsRing([0, 1, 2])
            pr2 = PsRing([3, 4, 5, 6])
            pr3 = PsRing([7])
            prev_xr = xrring.slots[1]
            sbf_cur = sbfring.next()
            for m in range(NCH):
                own = (m >= 2 and m % 2 == 0)
                ht = hntring.next()
                DMA(ht.t[:].rearrange("p a b -> p (a b)"), HNT[m], [], [ht.b], ht.sem)
                xr = xrring.next()
                CP("pool", xr.t[:, :, 0:3], prev_xr.t[:, :, 128:131], [prev_xr.b], [xr.b])
                for b3 in range(3):
                    px, Bpx = pr.next()
                    for cc in range(4):
                        ct = b3 * 4 + cc
                        MM([(px[:, cc * 128:(cc + 1) * 128], WB[:, k, XC + ct * 128: XC + (ct + 1) * 128],
                             ht.t[:, k, :], k == 0, k == 7) for k in range(8)], [ht.b] + BWB, [Bpx])
                    CP("act" if b3 != 1 else "dve", xr.t[:, b3 * 4:(b3 + 1) * 4, 3:131],
                       px[:, :].rearrange("p (a b) -> p a b", a=4), [Bpx], [xr.b])
                prev_xr = xr
                xt = xtring.next()
                bt = btring.next()
                for b3 in range(3):
                    pc, Bpc = pr.next()
                    ncc = 4 if b3 < 2 else 2
                    for cc in range(ncc):
                        ct = b3 * 4 + cc
                        MM([(pc[:, cc * 128:(cc + 1) * 128], xr.t[:, ct, k:k + 128], Wd[:, ct * 4 + k, :], k == 0, k == 3)
                            for k in range(4)], [xr.b] + SM + CONST, [Bpc])
                    cvt = cvring.next()
                    w_ = ncc * 128
                    TT("dve", cvt.t[:, 0:w_], pc[:, 0:w_], cbbc[:, b3 * 512: b3 * 512 + w_], ALU.add, [Bpc] + SM, [cvt.b])
                    if b3 < 2:
                        ACT(xt.t[:, b3 * 512:(b3 + 1) * 512], cvt.t[:, :], AF.Silu, [cvt.b], [xt.b])
                    else:
                        ACT(bt.t[:, :], cvt.t[:, 0:256], AF.Silu, [cvt.b], [bt.b])
                if own:
                    bct = bctring.next()
                    pf, Bpf = pr.next()
                    for cc in range(4):
                        ct = 8 + cc
                        MM([(pf[:, cc * 128:(cc + 1) * 128], Wd[:, ct * 4 + k, :], xr.t[:, ct, k:k + 128], k == 0, k == 3)
                            for k in range(4)], [xr.b] + SM, [Bpf])
                    for cc in range(4):
                        ACT(bct.t[:, cc, :], pf[:, cc * 128:(cc + 1) * 128], AF.Silu, [Bpf] + SM, [bct.b],
                            bias=cbcol[:, 8 + cc: 9 + cc])
                pd, Bpd = pr3.next()
                MM([(pd[:, 0:16], ht.t[:, k, :], WB[:, k, DTC:DTC + 16], k == 0, k == 7) for k in range(8)],
                   [ht.b] + BWB, [Bpd])
                dts = dtring.next()
                dtr, dte_, dtv, adt, w1 = (dts.t[:, 0:16], dts.t[:, 16:32], dts.t[:, 32:48], dts.t[:, 48:64],
                                           dts.t[:, 64:80])
                TT("dve", dtr, pd[:, 0:16], dtb, ALU.add, [Bpd] + SM, [dts.b])
                ACT(dte_, dtr, AF.Exp, [dts.b], [dts.b])
                ACT(dtv, dte_, AF.Ln, [dts.b], [dts.b], bias=1.0)
                TSM("dve", dtv, dtv, validt[:, m:m + 1], [dts.b] + SM, [dts.b])
                TT("dve", adt, dtv, aneg[:], ALU.mult, [dts.b] + SM, [dts.b])
                MM([(pd[:, 16:32], onesf[:], adt, True, True)], [dts.b] + CONST, [Bpd])
                MM([(pd[:, 32:48], trilef[:], adt, True, True)], [dts.b] + CONST, [Bpd])
                MM([(pd[:, 48:64], sgtf[:], adt, True, True)], [dts.b] + CONST, [Bpd])
                ex = exring.next()
                ACT(ex.t[:, 0:48], pd[:, 16:64], AF.Exp, [Bpd], [ex.b])
                cd, eacs, dte = ex.t[:, 0:16], ex.t[:, 16:32], ex.t[:, 32:48]
                TT("dve", w1, dtv, dte, ALU.mult, [dts.b, ex.b], [dts.b])
                xdtd = xdtdring.next()
                TT("pool", xdtd.t[:].rearrange("p (h q) -> p h q", h=16), xt.t[:].rearrange("p (h q) -> p h q", h=16),
                   w1.unsqueeze(2).to_broadcast([128, 16, 64]), ALU.mult, [xt.b, dts.b], [xdtd.b])
                if own:
                    jo = m // 2 - 1
                    gz = gzring.next()
                    gs = gsring.next()
                    for ng in range(4):
                        pz, Bpz = pr.next()
                        MM([(pz[:, :], ht.t[:, k, :], WB[:, k, ZS + ng * 512: ZS + (ng + 1) * 512], k == 0, k == 7)
                            for k in range(8)], [ht.b] + BWB, [Bpz])
                        if ng < 2:
                            ACT(gz.t[:, ng * 512:(ng + 1) * 512], pz[:, :], AF.Silu, [Bpz], [gz.b])
                        else:
                            ACT(gs.t[:, (ng - 2) * 512:(ng - 1) * 512], pz[:, :], AF.Silu, [Bpz], [gs.b])
                    DMA(GG[:, :, jo, :].rearrange("h t e -> t h e"), gs.t[:].rearrange("p (h e) -> p h e", h=8),
                        [gs.b], [], gs.sem)
                    xdt = xdtring.next()
                    TT("pool", xdt.t[:].rearrange("p (h q) -> p h q", h=16), xt.t[:].rearrange("p (h q) -> p h q", h=16),
                       dtv.unsqueeze(2).to_broadcast([128, 16, 64]), ALU.mult, [xt.b, dts.b], [xdt.b])
                    pcb, Bpcb = pr2.next()
                    for g2 in range(2):
                        MM([(pcb[:, g2 * 128:(g2 + 1) * 128], bct.t[:, g2, :], bct.t[:, 2 + g2, :], True, True)],
                           [bct.b], [Bpcb])
                    cbm = cbmring.next()
                    TT("dve", cbm.t[:], pcb[:, 0:256].rearrange("p (a b) -> p a b", a=2),
                       trilef[:].unsqueeze(1).to_broadcast([128, 2, 128]), ALU.mult, [Bpcb] + CONST, [cbm.b])
                    mt = mtring.next()
                    for b4 in range(4):
                        a4 = a4ring.next()
                        for hh in range(4):
                            hcol = adt[:, b4 * 4 + hh: b4 * 4 + hh + 1]
                            if hh % 2 == 0:
                                TSM("dve", a4.t[:, hh, :], sgtf[:], hcol, [dts.b] + CONST, [a4.b])
                            else:
                                ACT(a4.t[:, hh, :], sgtf[:], AF.Copy, [dts.b] + CONST, [a4.b], scale=hcol)
                        pe_, Bpe = pr2.next()
                        for hh in range(4):
                            MM([(pe_[:, hh * 128:(hh + 1) * 128], a4.t[:, hh, :], triler[:], True, True)],
                               [a4.b] + SM, [Bpe])
                        seg = segring.next()
                        ACT(seg.t[:].rearrange("p a b -> p (a b)"), pe_[:, :], AF.Exp, [Bpe], [seg.b])
                        TT("dve", mt.t[:, b4 * 4:(b4 + 1) * 4, :], seg.t[:],
                           cbm.t[:, b4 // 2, :].unsqueeze(1).to_broadcast([128, 4, 128]), ALU.mult,
                           [seg.b, cbm.b], [mt.b])
                    pys = []
                    for half in range(2):
                        py, Bpy = pr2.next()
                        for hh in range(8):
                            h = half * 8 + hh
                            MM([(py[:, hh * 64:(hh + 1) * 64], mt.t[:, h, :], xdt.t[:, h * 64:(h + 1) * 64], True, True)],
                               [mt.b, xdt.b], [Bpy])
                        pys.append((py, Bpy))
                    pos = []
                    for g2 in range(2):
                        po, Bpo = pr2.next()
                        MM([(po[:, :], bct.t[:, 2 + g2, :], sbf_cur.t[:, g2 * 512:(g2 + 1) * 512], True, True)],
                           [bct.b, sbf_cur.b], [Bpo])
                        pos.append((po, Bpo))
                    yo = yoring.next()
                    yy = yring.next()
                    y2 = y2ring.next()
                    for g2 in range(2):
                        sl_ = slice(g2 * 512, (g2 + 1) * 512)
                        TT("dve", yo.t[:, sl_].rearrange("p (h q) -> p h q", h=8),
                           pos[g2][0][:, :].rearrange("p (h q) -> p h q", h=8),
                           eacs[:, g2 * 8:(g2 + 1) * 8].unsqueeze(2).to_broadcast([128, 8, 64]), ALU.mult,
                           [pos[g2][1], ex.b], [yo.b])
                        TT("dve", yy.t[:, sl_], pys[g2][0][:, :], yo.t[:, sl_], ALU.add, [pys[g2][1], yo.b], [yy.b])
                    TT("pool", y2.t[:].rearrange("p (h q) -> p h q", h=16), xt.t[:].rearrange("p (h q) -> p h q", h=16),
                       dsk.unsqueeze(2).to_broadcast([128, 16, 64]), ALU.mult, [xt.b] + SM, [y2.b])
                    TT("pool", yy.t[:], yy.t[:], y2.t[:], ALU.add, [yy.b, y2.b], [yy.b])
                    TT("dve", yy.t[:], yy.t[:], gz.t[:], ALU.mult, [yy.b, gz.b], [yy.b])
                    for g2 in range(2):
                        ACT(junkb[:], yy.t[:, g2 * 512:(g2 + 1) * 512], AF.Square, [yy.b], [Bjunkb, Bst2a],
                            accum_out=st2[:, g2:g2 + 1])
                    RSQRT(st2[:, 0:2], st2[:, 2:4], st2[:, 4:6], 512, Bst2a, Bst2b, Bst2c)
                    ys = ysring.next()
                    for g2 in range(2):
                        sl_ = slice(g2 * 512, (g2 + 1) * 512)
                        STT("dve" if g2 == 0 else "pool", ys.t[:, sl_], yy.t[:, sl_], st2[:, 4 + g2:5 + g2], ssdg[:, sl_],
                            ALU.mult, ALU.mult, [yy.b, Bst2c] + SM, [ys.b])
                    DMA(YS[jo], ys.t[:], [ys.b], [], ys.sem)
                psts = []
                for g2 in range(2):
                    pst, Bpst = pr2.next()
                    MM([(pst[:, :], bt.t[:, g2 * 128:(g2 + 1) * 128], xdtd.t[:, g2 * 512:(g2 + 1) * 512], True, True)],
                       [bt.b, xdtd.b], [Bpst])
                    psts.append((pst, Bpst))
                TT("dve", state[:].rearrange("p (h q) -> p h q", h=16), state[:].rearrange("p (h q) -> p h q", h=16),
                   cd.unsqueeze(2).to_broadcast([128, 16, 64]), ALU.mult, [Bstate, ex.b], [Bstate])
                for g2 in range(2):
                    sl_ = slice(g2 * 512, (g2 + 1) * 512)
                    TT("dve", state[:, sl_], state[:, sl_], psts[g2][0][:, :], ALU.add, [Bstate, psts[g2][1]], [Bstate])
                nxt = m + 1
                if nxt < NCH and nxt >= 2 and nxt % 2 == 0:
                    sbf_cur = sbfring.next()
                    CP("pool", sbf_cur.t[:], state[:], [Bstate], [sbf_cur.b])
            S.barrier()
            S.emit()
        pab.close()

        with ExitStack() as pcx:
            S.lat_pad = 0.0
            smallc = sb(pcx, "smallC", [128, 128 + 256 + 8], F32)
            gnb = smallc[:, 0:128]
            lamv = smallc[:, 128:384]
            lamt = smallc[:, 384:392]
            Bsc = S.buf("smallC")
            scsem = S.dma_sem("smallC")
            DMAS([(gnb, attg_d), (lamv, lamv_d)], [], [Bsc], scsem)
            Bsc2 = S.buf("smallC2")
            S.op("dve", lambda e: e.tensor_scalar_mul(out=gnb, in0=gnb, scalar1=0.8), [Bsc], [Bsc])
            lprod = sb(pcx, "lprod", [128, 128], F32)
            TT("dve", lprod[:, 0:64], lamv[:, 0:64], lamv[:, 64:128], ALU.mult, [Bsc], [Bsc2])
            TT("dve", lprod[:, 64:128], lamv[:, 128:192], lamv[:, 192:256], ALU.mult, [Bsc], [Bsc2])
            junkc = sb(pcx, "junkC", [128, 128], F32)
            ACT(junkc[:, 0:64], lprod[:, 0:64], AF.Copy, [Bsc2], [Bsc2], accum_out=lamt[:, 0:1])
            ACT(junkc[:, 64:128], lprod[:, 64:128], AF.Copy, [Bsc2], [Bsc2], accum_out=lamt[:, 1:2])
            ACT(lamt[:, 2:4], lamt[:, 0:2], AF.Exp, [Bsc2], [Bsc2])
            TT("dve", lamt[:, 4:5], lamt[:, 3:4], lamt[:, 2:3], ALU.subtract, [Bsc2], [Bsc2])
            S.op("dve", lambda e: e.tensor_scalar_add(out=lamt[:, 5:6], in0=lamt[:, 4:5], scalar1=-0.2), [Bsc2], [Bsc2])
            lamneg = lamt[:, 5:6]
            SC = [Bsc, Bsc2]

            KTb = [[sb(pcx, "KTb%d%d" % (p, j), [66, T], BF16) for j in range(2)] for p in range(2)]
            Vb = [sb(pcx, "Vb%d" % p, [128, NCH, 129], BF16) for p in range(2)]
            QTb = [[sb(pcx, "QTb%d%d" % (p, j), [66, TO], BF16) for j in range(2)] for p in range(2)]
            Gb = [sb(pcx, "Gb%d" % p, [128, NOWN, 128], BF16) for p in range(2)]
            KBb = [sb(pcx, "KBb%d" % p, [128, NG * NCH], F32) for p in range(2)]
            Bset = [S.buf("set%d" % p) for p in range(2)]
            setsem = [S.dma_sem("set%d" % p) for p in range(2)]
            BsetV = [S.buf("setV%d" % p) for p in range(2)]
            setsemV = [S.dma_sem("setV%d" % p) for p in range(2)]
            ptring = mkring(pcx, "pt", 4, [128, 2, 512], BF16)
            osb = sb(pcx, "osb", [128, 9, 160], F32)
            Bosb = S.buf("osb")
            fin = sb(pcx, "fin", [128, 32], F32)
            Bfin = [S.buf() for _ in range(5)]
            tmpo = sb(pcx, "tmpo", [128, 4, 128], F32)
            oo = sb(pcx, "oo", [128, 4, 128], F32)
            o2 = sb(pcx, "o2", [128, 4, 128], F32)
            Btmpo, Boo, Bo2 = S.buf(), S.buf(), S.buf()
            yaring = mkring(pcx, "yas", 2, [128, 4, 128], BF16, dma=True)
            spairs = [(2, 4, 5), (3, 6, 7)]
            spi = [0]

            def oacc(i, j):
                idx = i * 2 + j
                return psb[idx // 3][:, (idx % 3) * 160:(idx % 3) * 160 + 129], idx // 3

            def load_head(h):
                p = h % 2
                pairs = []
                for j in range(2):
                    pairs.append((KTb[p][j][0:64, :], KT[h, j * 64:(j + 1) * 64, :]))
                    pairs.append((QTb[p][j][0:64, :], QT[h, j * 64:(j + 1) * 64, :]))
                    pairs.append((QTb[p][j][64:66, :], qaug_d[h]))
                    pairs.append((KTb[p][j][64:66, :], ones_d))
                pairs.append((KBb[p][:], kbias_d[:, h * NG * NCH:(h + 1) * NG * NCH]))
                DMAS(pairs, [], [Bset[p]], setsem[p])
                DMAS([(Vb[p][:], VV[h]), (Gb[p][:], GG[h])], [], [BsetV[p]], setsemV[p])

            load_head(0)
            for h in range(8):
                p = h % 2
                if h + 1 < 8:
                    load_head(h + 1)
                for g in range(NG):
                    m0 = 8 * g + 2
                    steps = []
                    for c in range(m0 + 7):
                        i0 = 0 if c <= m0 else (c - m0 + 1) // 2
                        steps.append((c, i0))
                    LA = 2
                    pend = []
                    started = set()
                    for idx in range(len(steps) + LA):
                        if idx < len(steps):
                            c, i0 = steps[idx]
                            ncol = (4 - i0) * 128
                            big, b0, b1 = spairs[spi[0] % 2]
                            spi[0] += 1
                            MM([(psb[b0][:, 0:ncol] if j == 0 else psb[b1][:, 0:ncol],
                                 KTb[p][j][0:66, c * 128:(c + 1) * 128],
                                 QTb[p][j][0:66, g * 512 + i0 * 128:(g + 1) * 512], True, True) for j in range(2)],
                               [Bset[p]], [Bps[b0], Bps[b1]])
                            pt = ptring.next()
                            ACT(pt.t[:, :, 0:ncol], psbig[big][:, :].rearrange("p (j c) -> p j c", j=2)[:, :, 0:ncol],
                                AF.Exp, [Bps[b0], Bps[b1], Bset[p]], [pt.b], scale=0.125,
                                bias=KBb[p][:, g * NCH + c: g * NCH + c + 1])
                            if c >= m0 and (c - m0) % 2 == 0:
                                TT("pool", pt.t[:, :, 0:128], pt.t[:, :, 0:128],
                                   trileb[:].unsqueeze(1).to_broadcast([128, 2, 128]), ALU.mult, [pt.b] + CONST, [pt.b])
                            pend.append((c, i0, pt))
                        if idx >= LA:
                            c, i0, pt = pend[idx - LA]
                            mms = []
                            wb = set()
                            for j in range(2):
                                for ii in range(4 - i0):
                                    i = i0 + ii
                                    oap, bk = oacc(i, j)
                                    first = bk not in started
                                    started.add(bk)
                                    mms.append((oap, pt.t[:, j, ii * 128:(ii + 1) * 128], Vb[p][:, c, 0:129], first, True, True))
                                    wb.add(Bps[bk])
                            MM(mms, [pt.b, BsetV[p]], list(wb))
                    for bk in range(3):
                        CP("dve", osb[:, bk * 3:(bk + 1) * 3, :].rearrange("p a b -> p (a b)"), psb[bk][:, 0:480],
                           [Bps[bk]], [Bosb])
                    rl = fin[:, 0:8]
                    rln = fin[:, 8:12]
                    ssq = fin[:, 12:16]
                    lnv = fin[:, 16:20]
                    rs = fin[:, 20:24]
                    S.op("dve", lambda e: e.reciprocal(out=rl.unsqueeze(2), in_=osb[:, 0:8, 128:129]), [Bosb], [Bfin[0]])
                    TSM("dve", rln, rl.rearrange("p (a b) -> p a b", b=2)[:, :, 1], lamneg, [Bfin[0]] + SC, [Bfin[1]])
                    for i in range(4):
                        TSM("dve", tmpo[:, i, :], osb[:, 2 * i, 0:128], rl[:, 2 * i:2 * i + 1], [Bosb, Bfin[0]], [Btmpo])
                        STT("dve", oo[:, i, :], osb[:, 2 * i + 1, 0:128], rln[:, i:i + 1], tmpo[:, i, :], ALU.mult, ALU.add,
                            [Bosb, Bfin[1], Btmpo], [Boo])
                    TT("dve", tmpo[:], oo[:], oo[:], ALU.mult, [Boo], [Btmpo])
                    S.op("dve", lambda e: e.reduce_sum(out=ssq, in_=tmpo[:], axis=mybir.AxisListType.X), [Btmpo], [Bfin[2]], dur=0.6, lat=6.0)
                    RSQRT(ssq, lnv, rs, 128, Bfin[2], Bfin[3], Bfin[4])
                    for i in range(4):
                        STT("dve", o2[:, i, :], oo[:, i, :], rs[:, i:i + 1], gnb, ALU.mult, ALU.mult,
                            [Boo, Bfin[4]] + SC, [Bo2])
                    ya = yaring.next()
                    TT("pool", ya.t[:], o2[:], Gb[p][:, 4 * g:4 * g + 4, :], ALU.mult, [Bo2, BsetV[p]], [ya.b])
                    DMA(YA[4 * g:4 * g + 4, :, h * 128:(h + 1) * 128].rearrange("i t e -> t i e"), ya.t[:], [ya.b], [],
                        ya.sem)
            S.barrier()
            S.emit()

        with ExitStack() as pd_:
            WO = sb(pd_, "WO", [128, 16, 1024], BF16)
            BWO = [S.buf("WO%d" % k) for k in range(16)]
            wosem = S.dma_sem("woload")
            for q4 in range(4):
                DMAS([(WO[:, kt, :], WOS[kt]) for kt in range(q4 * 4, q4 * 4 + 4)], [], BWO[q4 * 4:q4 * 4 + 4], wosem)
            fng = sb(pd_, "fng", [128, 1024], F32)
            Bfng = S.buf("fng")
            fsem = S.dma_sem("fng")
            DMA(fng[:], fng_d, [], [Bfng], fsem)
            ysl = mkring(pd_, "ysl", 3, [128, 2048], BF16, dma=True)
            xol = mkring(pd_, "xol", 3, [128, 1024], F32, dma=True)
            ytr = mkring(pd_, "ytr", 3, [128, 16, 128], BF16)
            hs = mkring(pd_, "hs", 2, [128, 1024], F32)
            outr = mkring(pd_, "outr", 2, [128, 1024], F32, dma=True)
            junkd = sb(pd_, "junkD", [128, 1024], BF16)
            Bjunkd = S.buf()
            st3 = sb(pd_, "stD", [128, 4], F32)
            Bs3 = [S.buf() for _ in range(3)]
            pr = PsRing(list(range(8)))
            def prepD(jo):
                m = 2 * jo + 2
                yl = ysl.next()
                DMAS([(yl.t[:, 0:1024], YS[jo]), (yl.t[:, 1024:2048], YA[jo])], [], [yl.b], yl.sem)
                xo = xol.next()
                DMA(xo.t[:], xl[m * 128:(m + 1) * 128, :], [], [xo.b], xo.sem)
                yt = ytr.next()
                for half in range(2):
                    pt_, Bpt_ = pr.next()
                    ptb = pt_.bitcast(BF16)
                    TR([(ptb[:, k * 128:(k + 1) * 128], yl.t[:, (half * 8 + k) * 128:(half * 8 + k + 1) * 128], identb[:])
                        for k in range(8)], [yl.b] + CONST, [Bpt_])
                    CP("act" if half == 0 else "dve", yt.t[:, half * 8:(half + 1) * 8, :].rearrange("p a b -> p (a b)"),
                       ptb[:, 0:1024], [Bpt_], [yt.b])
                return yt, xo

            def mainD(jo, yt, xo):
                hsum = hs.next()
                for ng in range(2):
                    po, Bpo = pr.next()
                    MM([(po[:, :], yt.t[:, k, :], WO[:, k, ng * 512:(ng + 1) * 512], k == 0, k == 15) for k in range(16)],
                       [yt.b] + BWO, [Bpo])
                    TT("dve", hsum.t[:, ng * 512:(ng + 1) * 512], po[:, :], xo.t[:, ng * 512:(ng + 1) * 512], ALU.add,
                       [Bpo, xo.b], [hsum.b])
                ACT(junkd[:], hsum.t[:], AF.Square, [hsum.b], [Bjunkd, Bs3[0]], accum_out=st3[:, 0:1])
                RSQRT(st3[:, 0:1], st3[:, 1:2], st3[:, 2:3], D, Bs3[0], Bs3[1], Bs3[2])
                ot = outr.next()
                STT("dve", ot.t[:, 0:512], hsum.t[:, 0:512], st3[:, 2:3], fng[:, 0:512], ALU.mult, ALU.mult,
                    [hsum.b, Bs3[2], Bfng], [ot.b])
                STT("dve", ot.t[:, 512:1024], hsum.t[:, 512:1024], st3[:, 2:3], fng[:, 512:1024], ALU.mult, ALU.mult,
                    [hsum.b, Bs3[2], Bfng], [ot.b])
                DMA(out_d[jo * 128:(jo + 1) * 128, :], ot.t[:], [ot.b], [], ot.sem)

            prepped = {}
            for jo in range(NOWN + 1):
                if jo < NOWN:
                    prepped[jo] = prepD(jo)
                if jo >= 1:
                    mainD(jo - 1, *prepped.pop(jo - 1))
            S.final_wait("sp")
            S.emit()
    return nc


def _bc(v, n=128):
    return np.ascontiguousarray(np.broadcast_to(np.asarray(v, np.float32).reshape(1, -1), (n, np.size(v))))


def prep(inputs, NXC):
    NCH = NXC + 1
    NOWN = NXC // 2
    NG = NOWN // 4
    f = lambda a: np.asarray(a, np.float32)
    x = f(inputs["x"])
    meta = f(inputs["meta"])
    w_in = f(inputs["w_in"])[0]
    zs, xbc, dtc = w_in[:, 0:1024], w_in[:, 1024:2560], w_in[:, 2560:2576]
    q, k, v, za = w_in[:, 2576:3600], w_in[:, 3600:4624], w_in[:, 4624:5648], w_in[:, 5648:6672]
    wA = np.ascontiguousarray(np.concatenate([k, v, q], axis=1))
    wB = np.ascontiguousarray(np.concatenate([xbc, dtc, zs, za], axis=1))
    wo = np.ascontiguousarray(f(inputs["w_out"])[0])
    gbc = _bc(f(inputs["norm_g"])[0])
    cwf = f(inputs["conv_w"])[0]
    cw = np.ascontiguousarray(cwf.reshape(4, 12, 128).transpose(2, 1, 0).reshape(128, 48))
    cb = f(inputs["conv_b"])[0]
    cbbc = _bc(cb[0:1280])
    cbcol = np.ascontiguousarray(cb.reshape(12, 128).T)
    vecs = np.concatenate([_bc(f(inputs["dt_bias"])[0]), _bc(f(inputs["a_log"])[0]), _bc(f(inputs["d_skip"])[0])], axis=1)
    ssdg = _bc(f(inputs["ssd_norm_g"])[0])
    attg = _bc(f(inputs["attn_norm_g"])[0])
    fng = _bc(f(inputs["final_norm_g"]))
    lamv = np.concatenate([_bc(f(inputs[n])[0]) for n in ("lambda_q1", "lambda_k1", "lambda_q2", "lambda_k2")], axis=1)
    ident = np.eye(128, dtype=np.float32)
    ar = np.arange(128)
    trile = (ar[:, None] <= ar[None, :]).astype(np.float32)
    sgt = (ar[:, None] > ar[None, :]).astype(np.float32)
    slopes = 2.0 ** (-8.0 * np.arange(1, 9) / 8.0)
    col = np.arange(512)
    qaug = np.zeros((8, 2, 512), np.float32)
    for h in range(8):
        qaug[h, 0] = -slopes[h] * 8.0 * (col % 128)
        qaug[h, 1] = -slopes[h] * 8.0 * 256.0 * (col // 128)
    import ml_dtypes
    qaug = np.ascontiguousarray(np.tile(qaug, (1, 1, NG)).astype(ml_dtypes.bfloat16))
    onesk = np.ones((2, NCH * 128), dtype=ml_dtypes.bfloat16)
    shared = dict(onesk=onesk, wA=wA, wB=wB, wo=wo, gbc=gbc, cw=cw, cbbc=cbbc, cbcol=cbcol, vecs=np.ascontiguousarray(vecs),
                  ssdg=ssdg, attg=attg, fng=fng, lamv=np.ascontiguousarray(lamv), ident=ident, trile=trile, sgt=sgt,
                  qaug=qaug)
    in_maps = []
    B = x.shape[0]
    for b in range(B):
        for r in range(2):
            if r == 1:
                xl = np.concatenate([np.zeros((N_PAD, D), np.float32), meta, x[b, 0:NXC * 128]], axis=0)
                nvalid0 = N_PAD
            else:
                xl = np.concatenate([np.zeros((128 + N_PAD, D), np.float32), meta, x[b, 0:(NXC - 1) * 128]], axis=0)
                nvalid0 = 128 + N_PAD
            pos = np.arange(NCH * 128)
            val = (pos >= nvalid0).astype(np.float32)
            valid = np.ascontiguousarray(val.reshape(NCH, 128).T)
            kb = np.zeros((128, 8, NG, NCH), np.float32)
            for g in range(NG):
                m0 = 8 * g + 2
                for c in range(NCH):
                    base = ar[:, None] - 128.0 * (m0 - c)
                    kb[:, :, g, c] = base * slopes[None, :] + np.where(val[c * 128 + ar] > 0, 0.0, NEGBIG)[:, None]
            d = dict(shared)
            d["xl"] = np.ascontiguousarray(xl)
            d["valid"] = valid
            d["kbias"] = np.ascontiguousarray(kb.reshape(128, 8 * NG * NCH))
            in_maps.append(d)
    return in_maps


def assemble(results, B, NXC):
    NOWN = NXC // 2
    out = np.zeros((B, NXC * 128, D), np.float32)
    for b in range(B):
        for r in range(2):
            o = results[b * 2 + r]["out"]
            for jo in range(NOWN):
                xc = 2 * jo + r
                out[b, xc * 128:(xc + 1) * 128] = o[jo * 128:(jo + 1) * 128]
    return out


def kernel(**inputs):
    x = np.asarray(inputs["x"])
    B, SEQ, _ = x.shape
    NXC = SEQ // 128
    nc = build(NXC)
    in_maps = prep(inputs, NXC)
    res = run_bass_kernel_spmd(nc, in_maps, core_ids=list(range(2 * B)))
    return assemble(res.results, B, NXC)
```
